# Optimizing a Trainium2 kernel written in Bass

```python
import math
import jax, jax.numpy as jnp
from jax import lax
import numpy as np

D_MODEL = 1024
BATCH = 1
SEQ = 16384
DEPTH = 1

N_MEM = 256
D_MIX = D_MODEL
D_A = D_MIX // 2
D_B = D_MIX - D_A
CONV_A_W = 3
CONV_B_W = 31
D_IN_ALL = 3 * D_A + 2 * D_B
XA_HEADS = 4
XA_HEAD_DIM = D_MODEL // XA_HEADS
D_FF = int(math.ceil((8 * D_MODEL / 3) / 256) * 256)
RMS_EPS = 1e-6
LN_EPS = 1e-5

kernel_name = "hybrid_parallel_conv_groups_xattn_swiglu"


def rmsnorm(x, g):
    xf = x.astype(jnp.float32)
    y = xf * lax.rsqrt(jnp.mean(xf * xf, axis=-1, keepdims=True) + RMS_EPS)
    return (y * g.astype(jnp.float32)).astype(x.dtype)


def layernorm(x, g, b):
    xf = x.astype(jnp.float32)
    mu = jnp.mean(xf, axis=-1, keepdims=True)
    var = jnp.mean(jnp.square(xf - mu), axis=-1, keepdims=True)
    y = (xf - mu) * lax.rsqrt(var + LN_EPS)
    return (y * g.astype(jnp.float32) + b.astype(jnp.float32)).astype(x.dtype)


def causal_dwconv(u, w):
    k = w.shape[0]
    return lax.conv_general_dilated(
        u, w[:, None, :].astype(u.dtype),
        window_strides=(1,), padding=((k - 1, 0),),
        dimension_numbers=("NWC", "WIO", "NWC"),
        feature_group_count=u.shape[-1])


def setup_inputs(seed: int = 0) -> dict:
    key = jax.random.key(seed)
    ks = jax.random.split(key, 24)
    f32 = jnp.float32

    def w(k, shape, fan_in):
        return jax.random.normal(k, shape, f32) * (fan_in ** -0.5)

    def gain(k, n):
        return jnp.ones((n,), f32) + 0.05 * jax.random.normal(k, (n,), f32)

    return {
        "x": jax.random.normal(ks[0], (BATCH, SEQ, D_MODEL), f32),
        "mem": jax.random.normal(ks[1], (BATCH, N_MEM, D_MODEL), f32),
        "mix_pre_g": gain(ks[2], D_MODEL),
        "w_mix_in": w(ks[3], (D_MODEL, D_IN_ALL), D_MODEL),
        "conv_a_w": w(ks[4], (CONV_A_W, D_A), CONV_A_W),
        "conv_b_w": w(ks[5], (CONV_B_W, D_B), CONV_B_W),
        "conv_b_b": 0.02 * jax.random.normal(ks[6], (D_B,), f32),
        "ln_b_g": gain(ks[7], D_B),
        "ln_b_b": 0.02 * jax.random.normal(ks[8], (D_B,), f32),
        "w_mix_out": w(ks[9], (D_MIX, D_MODEL), D_MIX),
        "mix_post_g": gain(ks[10], D_MODEL),
        "xa_pre_g": gain(ks[11], D_MODEL),
        "mem_norm_g": gain(ks[12], D_MODEL),
        "w_q": w(ks[13], (D_MODEL, XA_HEADS * XA_HEAD_DIM), D_MODEL),
        "w_k": w(ks[14], (D_MODEL, XA_HEADS * XA_HEAD_DIM), D_MODEL),
        "w_v": w(ks[15], (D_MODEL, XA_HEADS * XA_HEAD_DIM), D_MODEL),
        "w_o": w(ks[16], (XA_HEADS * XA_HEAD_DIM, D_MODEL), XA_HEADS * XA_HEAD_DIM),
        "xa_post_g": gain(ks[17], D_MODEL),
        "ffn_pre_g": gain(ks[18], D_MODEL),
        "w_gate": w(ks[19], (D_MODEL, D_FF), D_MODEL),
        "w_up": w(ks[20], (D_MODEL, D_FF), D_MODEL),
        "w_down": w(ks[21], (D_FF, D_MODEL), D_FF),
        "ffn_post_g": gain(ks[22], D_MODEL),
    }


def parallel_conv_mixer(h, w_mix_in, conv_a_w, conv_b_w, conv_b_b, ln_b_g, ln_b_b, w_mix_out):
    u = jnp.einsum("bsd,dc->bsc", h, w_mix_in.astype(h.dtype))
    b_a, c_a, v_a, glu_v, glu_g = jnp.split(
        u, [D_A, 2 * D_A, 3 * D_A, 3 * D_A + D_B], axis=-1)
    y_a = b_a * causal_dwconv(c_a * v_a, conv_a_w)
    z = glu_v * jax.nn.sigmoid(glu_g)
    z = causal_dwconv(z, conv_b_w) + conv_b_b.astype(z.dtype)
    y_b = jax.nn.silu(layernorm(z, ln_b_g, ln_b_b))
    y = jnp.concatenate([y_a, y_b], axis=-1)
    return jnp.einsum("bsc,cd->bsd", y, w_mix_out.astype(y.dtype))


def memory_cross_attention(h, mem_n, w_q, w_k, w_v, w_o):
    bsz, s, _ = h.shape
    m = mem_n.shape[1]
    q = jnp.einsum("bsd,de->bse", h, w_q.astype(h.dtype)).reshape(bsz, s, XA_HEADS, XA_HEAD_DIM)
    k = jnp.einsum("bmd,de->bme", mem_n, w_k.astype(h.dtype)).reshape(bsz, m, XA_HEADS, XA_HEAD_DIM)
    v = jnp.einsum("bmd,de->bme", mem_n, w_v.astype(h.dtype)).reshape(bsz, m, XA_HEADS, XA_HEAD_DIM)
    scores = jnp.einsum("bshd,bmhd->bhsm", q.astype(jnp.float32), k.astype(jnp.float32))
    p = jax.nn.softmax(scores * (XA_HEAD_DIM ** -0.5), axis=-1).astype(h.dtype)
    o = jnp.einsum("bhsm,bmhd->bshd", p, v).reshape(bsz, s, XA_HEADS * XA_HEAD_DIM)
    return jnp.einsum("bse,ed->bsd", o, w_o.astype(h.dtype))


def swiglu_ffn(h, w_gate, w_up, w_down):
    g = jnp.einsum("bsd,df->bsf", h, w_gate.astype(h.dtype))
    u = jnp.einsum("bsd,df->bsf", h, w_up.astype(h.dtype))
    return jnp.einsum("bsf,fd->bsd", jax.nn.silu(g) * u, w_down.astype(h.dtype))


def reference(x, mem, mix_pre_g, w_mix_in, conv_a_w, conv_b_w, conv_b_b, ln_b_g, ln_b_b,
              w_mix_out, mix_post_g, xa_pre_g, mem_norm_g, w_q, w_k, w_v, w_o, xa_post_g,
              ffn_pre_g, w_gate, w_up, w_down, ffn_post_g):
    mem_n = rmsnorm(mem, mem_norm_g)
    for _ in range(DEPTH):
        x = x + rmsnorm(parallel_conv_mixer(rmsnorm(x, mix_pre_g), w_mix_in, conv_a_w,
                                            conv_b_w, conv_b_b, ln_b_g, ln_b_b, w_mix_out),
                        mix_post_g)
        x = x + rmsnorm(memory_cross_attention(rmsnorm(x, xa_pre_g), mem_n, w_q, w_k, w_v, w_o),
                        xa_post_g)
        x = x + rmsnorm(swiglu_ffn(rmsnorm(x, ffn_pre_g), w_gate, w_up, w_down), ffn_post_g)
    return x
```

```python
from contextlib import ExitStack
import numpy as np
import concourse.bass as bass
import concourse.mybir as mybir
from concourse.bass_utils import run_bass_kernel_spmd

F32 = mybir.dt.float32
BF16 = mybir.dt.bfloat16
ALU = mybir.AluOpType
AF = mybir.ActivationFunctionType

NCORES = 8
D = 1024
SEQ = 16384
TPC = SEQ // NCORES
TT = 256
NT = TPC // TT
HALO = 32
NMEM = 256
DFF = 2816
NJ = DFF // 128
DIN = 2560
RMS_EPS = 1e-6
LN_EPS = 1e-5

SM_G = 0
SM_MEMG = 48
SM_CBB = 56
SM_LNG = 60
SM_LNB = 64
SM_CAW = 68
SM_CBW = 80
SM_N = 204

ENGS = ["pe", "act", "dve", "pool", "sp"]
NPHASES = 3
SB_BASE = 17408


class Sched:
    def __init__(self, nc, stack):
        self.nc = nc
        self.stack = stack
        self.lists = {e: [] for e in ENGS}
        self.count = {e: 0 for e in ENGS}
        self.waited = {e: {} for e in ENGS}
        self.sems = {}
        self.dma_cum = {}
        self.bufs = {}
        self.known = {e: {} for e in ENGS}
        self.snap = {}
        self.seq = {}
        self.nrec = 0
        self.cur = ""
        for e in ENGS:
            self.sems[e] = stack.enter_context(nc.semaphore("s_" + e))

    def dma_sem(self, name):
        key = "dma_" + name
        if key not in self.sems:
            self.sems[key] = self.stack.enter_context(self.nc.semaphore(key))
            self.dma_cum[key] = 0
        return key

    def _deps(self, eng, reads, writes):
        toks = []
        for k in reads:
            st = self.bufs.get(k)
            if st and st["w"] is not None:
                toks.append(st["w"])
        for k in writes:
            st = self.bufs.get(k)
            if st:
                if st["w"] is not None:
                    toks.append(st["w"])
                toks.extend(st["r"].items())
        need = {}
        for (sk, v) in toks:
            if sk == "pe" and eng == "pe":
                continue
            if v > need.get(sk, 0):
                need[sk] = v
        waits = []
        kn = self.known[eng]
        for sk, v in sorted(need.items(), key=lambda it: -self.seq.get(it, 0)):
            if kn.get(sk, 0) >= v:
                continue
            waits.append((sk, v))
            kn[sk] = v
            sn = self.snap.get((sk, v))
            if sn:
                for k2, v2 in sn.items():
                    if v2 > kn.get(k2, 0):
                        kn[k2] = v2
        return waits

    def _update(self, tok, reads, writes):
        for k in reads:
            st = self.bufs.setdefault(k, {"w": None, "r": {}})
            if tok[1] > st["r"].get(tok[0], 0):
                st["r"][tok[0]] = tok[1]
        for k in writes:
            self.bufs[k] = {"w": tok, "r": {}}

    def alias(self, newkeys, oldkeys):
        merged = {}
        for k in oldkeys:
            st = self.bufs.get(k)
            if not st:
                continue
            items = list(st["r"].items())
            if st["w"] is not None:
                items.append(st["w"])
            for sk, v in items:
                if v > merged.get(sk, 0):
                    merged[sk] = v
        for k in newkeys:
            st = self.bufs.setdefault(k, {"w": None, "r": {}})
            for sk, v in merged.items():
                if v > st["r"].get(sk, 0):
                    st["r"][sk] = v

    def op(self, eng, fn, reads=(), writes=(), signal=True):
        waits = self._deps(eng, reads, writes)
        if signal:
            self.count[eng] += 1
            tok = (eng, self.count[eng])
            inc = (eng, 1)
            sn = dict(self.known[eng])
            sn.pop(eng, None)
            self.snap[tok] = sn
            self.nrec += 1
            self.seq[tok] = self.nrec
        else:
            tok = (eng, self.count[eng] + 1)
            inc = None
        self.lists[eng].append((waits, fn, inc, self.cur))
        self._update(tok, reads, writes)
        return tok

    def dma(self, eng, semname, fn, reads=(), writes=()):
        key = self.dma_sem(semname)
        waits = self._deps(eng, reads, writes)
        self.dma_cum[key] += 16
        tok = (key, self.dma_cum[key])
        sn = dict(self.known[eng])
        sn.pop(eng, None)
        self.snap[tok] = sn
        self.nrec += 1
        self.seq[tok] = self.nrec
        self.lists[eng].append((waits, fn, (key, 16), self.cur))
        self._update(tok, reads, writes)
        return tok

    def wait_all(self, eng, toks):
        need = {}
        for (sk, v) in toks:
            if v > need.get(sk, 0):
                need[sk] = v
        self.lists[eng].append((list(need.items()), None, None, self.cur))

    def emit(self, block):
        sems = self.sems
        lists = self.lists

        def run(engname):
            def body(e):
                for (waits, fn, inc, _lbl) in lists[engname]:
                    for (sk, v) in waits:
                        e.wait_ge(sems[sk], v)
                    if fn is not None:
                        ins = fn(e)
                        if inc is not None:
                            ins.then_inc(sems[inc[0]], inc[1])
            return body

        block.tensor(run("pe"))
        block.scalar(run("act"))
        block.vector(run("dve"))
        block.gpsimd(run("pool"))
        block.sync(run("sp"))


def build_program():
    nc = bass.Bass("TRN2", target_bir_lowering=False)
    dr = lambda name, shape, kind="ExternalInput": nc.dram_tensor(name, shape, F32, kind=kind).ap()
    xT_d = dr("xT", [128, 8, HALO + TPC])
    small_d = dr("small", [128, SM_N])
    ident_d = dr("ident", [128, 128])
    memT_d = dr("memT", [128, 8, NMEM])
    w_in_d = dr("w_mix_in", [D, DIN])
    w_out_d = dr("w_mix_out", [D, D])
    wq_d = dr("w_q", [D, D])
    wk_d = dr("w_k", [D, D])
    wv_d = dr("w_v", [D, D])
    wo_d = dr("w_o", [D, D])
    wg_d = dr("w_gate", [D, DFF])
    wu_d = dr("w_up", [D, DFF])
    wd_d = dr("w_down", [DFF, D])
    outT_d = dr("outT", [128, 8, TPC], kind="ExternalOutput")
    wd_bf = nc.dram_tensor("wd_bf16", [DFF, D], BF16).ap()

    with ExitStack() as stack:
        S = Sched(nc, stack)
        global _LAST_SCHED
        _LAST_SCHED = S
        _n = [0]

        def sbt(shape, dt, off):
            _n[0] += 1
            nbytes = int(np.prod(shape[1:])) * (4 if dt == F32 else 2)
            assert SB_BASE + off + nbytes <= 229344, (shape, off)
            return nc.alloc_sbuf_tensor_at("t%d" % _n[0], list(shape), dt, offset=SB_BASE + off)

        X0 = 0
        XH0 = 65536
        SM0 = 66560
        ID0 = 67392
        ON0 = 67904
        NH0 = 68160
        KT0 = 69184
        V0 = 73280
        A0 = 77376
        W0 = 169536
        WSIZE = 229344 - SB_BASE - W0
        x_sb = sbt([128, 8, TPC], F32, X0)
        xh_sb = sbt([128, 8, HALO], F32, XH0)
        sm = sbt([128, SM_N], F32, SM0)
        ident = sbt([128, 128], F32, ID0)
        ones = sbt([128, 128], BF16, ON0)
        epsc = sbt([128, 2], F32, NH0)
        KT = sbt([128, 8, NMEM], BF16, KT0)
        Vt = sbt([128, 2, D], BF16, V0)
        w_in = sbt([128, 8, DIN], BF16, A0)
        w_out = sbt([128, 8, D], BF16, A0 + 40960)
        diag = sbt([128, 136, 128], BF16, A0 + 57344)
        wk = sbt([128, 8, D], BF16, A0)
        wv = sbt([128, 8, D], BF16, A0 + 16384)
        wq = sbt([128, 8, D], BF16, A0 + 59392)
        wo = sbt([128, 8, D], BF16, A0 + 75776)
        wg = sbt([128, 8, DFF], BF16, A0)
        wu = sbt([128, NJ, 8, 128], BF16, A0 + 45056)
        ms_pre = sbt([128, TT], F32, W0)
        rstd_pre = sbt([128, TT], F32, W0 + 1024)
        ms_post = sbt([128, TT], F32, W0 + 2048)
        rstd_post = sbt([128, TT], F32, W0 + 3072)
        hh = [sbt([128, 8, TT], BF16, W0 + 4096), sbt([128, 8, TT], BF16, W0 + 11264)]
        ysq = sbt([128, 2, TT], BF16, W0 + 8192)
        h_halo = sbt([128, 8, HALO], BF16, W0 + 8192)
        tmp = sbt([128, 2, TT], F32, W0 + 9216)
        WP = W0 + 11264
        WP3 = W0 + 15360
        sq8_1 = sbt([128, 8, TT], BF16, KT0)
        ycat_a = sbt([128, 2, 4, TT], BF16, V0)
        th = sbt([128, 2, TT], F32, WP)
        zbuf = sbt([128, 2, 4, HALO + TT], BF16, WP + 2048)
        zc = sbt([128, 4, TT], BF16, WP + 6656)
        zq = sbt([128, 4, TT], BF16, WP + 8704)
        mu = sbt([128, TT], F32, WP + 10752)
        v1 = sbt([128, TT], F32, WP + 11776)
        rstdB = sbt([128, TT], F32, WP + 12800)
        nmr = sbt([128, TT], F32, WP + 13824)
        t1 = sbt([128, 2, TT], F32, WP + 14848)
        t2 = sbt([128, 2, TT], F32, WP + 16896)
        ca = sbt([128, 2, TT], F32, WP + 18944)
        cat = sbt([128, 2, TT], F32, WP + 20992)
        cvbuf = sbt([128, 2, 4, HALO + TT], BF16, WP + 23040)
        ycat_b = sbt([128, 4, TT], BF16, WP + 27648)
        assert WP + 29696 - W0 <= WSIZE
        sq8_2 = sbt([128, 8, TT], BF16, WP)
        qT = sbt([128, 8, TT], BF16, WP + 4096)
        pbuf = sbt([128, 4, 2, TT], BF16, WP + 8192)
        rs = sbt([128, 4, TT], F32, WP + 12288)
        oT = sbt([128, 8, TT], BF16, WP + 16384)
        memT = sbt([128, 8, NMEM], F32, A0 + 32768)
        hm = sbt([128, 8, NMEM], BF16, WP + 12288)
        assert WP + 28672 - W0 <= WSIZE
        NS = 7
        sq8_3 = sbt([128, 8, TT], BF16, KT0)
        assert NS == 7
        actj = sbt([128, 4, TT], BF16, W0 + 15360)
        ysq8 = sbt([128, 8, TT], BF16, W0 + 17408)
        tmp4 = sbt([128, 4, TT], F32, W0 + 21504)
        wdjB = sbt([128, 3, D], BF16, W0 + 25600)
        wdjA = sbt([128, 4, D], BF16, W0 + 31744)
        sg = sbt([128, 2, TT], F32, W0 + 39936)
        assert W0 + 41984 - W0 <= WSIZE
        wd_slot = lambda s_: (wdjA[:, s_, :] if s_ < 4 else wdjB[:, s_ - 4, :])
        wd_slot_cols = lambda s_, c0, c1: (wdjA[:, s_, c0:c1] if s_ < 4 else wdjB[:, s_ - 4, c0:c1])

        ps = [stack.enter_context(nc.psum_tensor("ps%d" % i, [128, 512], F32)) for i in range(8)]

        def hb(bank, half, n=TT):
            return ps[bank][:, half * 256:half * 256 + n]

        def bk(bank):
            return "bk%d" % bank

        S.dma("sp", "small", lambda e: e.dma_start(out=sm[:], in_=small_d), writes=["small"])
        S.dma("sp", "ident", lambda e: e.dma_start(out=ident[:], in_=ident_d), writes=["ident"])
        S.dma("pool", "xh", lambda e: e.dma_start(out=xh_sb[:], in_=xT_d[:, :, 0:HALO]), writes=["xh"])
        S.dma("pool", "x0", lambda e: e.dma_start(out=x_sb[:, :, 0:TT], in_=xT_d[:, :, HALO:HALO + TT]),
              writes=["x%d_%d" % (c, 0) for c in range(8)])
        def load_group(semname, items):
            tok = None
            for (keys, fn) in items:
                keys = [keys] if isinstance(keys, str) else keys
                tok = S.dma("pool", semname, fn, writes=keys)
            for (keys, fn) in items:
                keys = [keys] if isinstance(keys, str) else keys
                for key in keys:
                    S.bufs[key]["w"] = tok

        def load_w(wsb, wdr, nm, c0=0, c1=None, keyp=None):
            c1 = wsb.shape[2] if c1 is None else c1
            keyp = nm if keyp is None else keyp
            src = wdr.rearrange("(kc p) n -> p kc n", p=128)
            items = []
            for half in range(2):
                k0, k1 = half * 4, half * 4 + 4
                fn = (lambda k0=k0, k1=k1: lambda e: e.dma_start(out=wsb[:, k0:k1, c0:c1], in_=src[:, k0:k1, c0:c1]))()
                items.append(([keyp + str(kc) for kc in range(k0, k1)], fn))
            load_group(nm, items)

        load_w(w_in, w_in_d, "w_in_cv", 512, 1536)
        load_w(w_in, w_in_d, "w_in_g", 1536, 2560)
        load_w(w_in, w_in_d, "w_in_b", 0, 512)
        S.dma("pool", "x1", lambda e: e.dma_start(out=x_sb[:, :, TT:2 * TT], in_=xT_d[:, :, HALO + TT:HALO + 2 * TT]),
              writes=["x%d_%d" % (c, 1) for c in range(8)])
        for t in range(2, NT):
            S.dma("sp", "x%d" % t,
                  (lambda t=t: lambda e: e.dma_start(out=x_sb[:, :, t * TT:(t + 1) * TT],
                                                     in_=xT_d[:, :, HALO + t * TT:HALO + (t + 1) * TT]))(),
                  reads=["w_in_b7"],
                  writes=["x%d_%d" % (c, t) for c in range(8)])

        S.op("dve", lambda e: e.memset(ones[:], 1.0), writes=["ones"])
        S.op("dve", lambda e: e.memset(epsc[:, 0:1], RMS_EPS), writes=["epsc"])
        S.op("dve", lambda e: e.memset(epsc[:, 1:2], LN_EPS), writes=["epsc"])
        S.op("dve", lambda e: e.tensor_scalar(out=sm[:, SM_CBW:SM_CBW + 124], in0=sm[:, SM_CBW:SM_CBW + 124],
                                              scalar1=0.5, scalar2=None, op0=ALU.mult),
             reads=["small"], writes=["small"])
        for part in range(4):
            S.op("dve", (lambda part=part: lambda e: e.tensor_tensor(
                out=diag[:, part * 34:(part + 1) * 34, :],
                in0=ident[:, :].unsqueeze(1).broadcast_to([128, 34, 128]),
                in1=sm[:, SM_CAW + part * 34:SM_CAW + (part + 1) * 34].unsqueeze(2).broadcast_to([128, 34, 128]),
                op=ALU.mult))(),
                reads=["small", "ident"], writes=["diag%d" % part])

        load_w(w_out, w_out_d, "w_out")
        load_group("wdcast", [("wdbf%d" % j, (lambda j=j: lambda e: e.dma_start(out=wd_bf[j * 128:(j + 1) * 128, :],
                                                                             in_=wd_d[j * 128:(j + 1) * 128, :]))())
                              for j in range(NJ)])
        wu_src = wu_d.rearrange("(kc p) n -> p kc n", p=128)

        def load_wu(j0, j1, semname):
            load_group(semname, [("wu%d" % j,
                                  (lambda j=j: lambda e: e.dma_start(out=wu[:, j, :, :],
                                                                     in_=wu_src[:, :, j * 128:(j + 1) * 128]))())
                                 for j in range(j0, j1)])

        NJ_ = NJ

        def load_wd(g):
            j = g % NJ_
            slot = g % NS
            S.dma("sp", "wdj%d" % slot,
                  (lambda j=j, slot=slot: lambda e: e.dma_start(out=wd_slot(slot),
                                                                in_=wd_bf[j * 128:(j + 1) * 128, :]))(),
                  reads=["wdbf%d" % j], writes=["wdj%d" % slot])

        def gcol(norm, c):
            return sm[:, SM_G + norm * 8 + c:SM_G + norm * 8 + c + 1]

        def PA(src_fn, src_keys, n, sq8):
            S.cur = "PA"
            for c in range(8):
                S.op("act", (lambda c=c: lambda e: e.activation(out=sq8[:, c, 0:n], in_=src_fn(c), func=AF.Square))(),
                     reads=[src_keys[c]], writes=["sq8_%d" % c])

        def PB(n, sq8):
            S.cur = "PB"
            for i, c in enumerate(reversed(range(8))):
                S.op("pe", (lambda c=c, i=i: lambda e: e.matmul(hb(7, 0, n), lhsT=ones[:], rhs=sq8[:, c, 0:n],
                                                               start=(i == 0), stop=(i == 7)))(),
                     reads=["ones", "sq8_%d" % c], writes=[bk(7)], signal=(i == 7))
            S.op("act", lambda e: e.activation(out=ms_pre[:, 0:n], in_=hb(7, 0, n), func=AF.Ln, scale=1.0 / D,
                                               bias=epsc[:, 0:1]),
                 reads=["epsc"], writes=["ms_pre", bk(7)])
            S.op("act", lambda e: e.activation(out=rstd_pre[:, 0:n], in_=ms_pre[:, 0:n], func=AF.Exp, scale=-0.5),
                 reads=["ms_pre"], writes=["rstd_pre"])

        def PC(src_fn, src_keys, n, norm, hp=0, halo=False):
            S.cur = "PC"
            for c in range(8):
                dst = h_halo[:, c, 0:n] if halo else hh[hp][:, c, 0:n]
                S.op("dve", (lambda c=c, dst=dst: lambda e: e.scalar_tensor_tensor(
                    out=dst, in0=src_fn(c), scalar=gcol(norm, c), in1=rstd_pre[:, 0:n],
                    op0=ALU.mult, op1=ALU.mult))(),
                    reads=[src_keys[c], "small", "rstd_pre"], writes=[("hh_%d" % c) if halo else ("h%d_%d" % (hp, c))])

        def proj(dst_bank, dst_half, n, w_sb, wkey, col0, rhs_fn, rhs_keys, nk=8, lhs_fn=None, wkeys=None, korder=None):
            if lhs_fn is None:
                lhs_fn = lambda kc: w_sb[:, kc, col0:col0 + 128]
            if wkeys is None:
                wkeys = [wkey + str(kc) for kc in range(nk)]
            korder = list(range(nk)) if korder is None else list(korder)
            for i, kc in enumerate(korder):
                S.op("pe", (lambda kc=kc, i=i: lambda e: e.matmul(
                    hb(dst_bank, dst_half, n), lhsT=lhs_fn(kc), rhs=rhs_fn(kc),
                    start=(i == 0), stop=(i == nk - 1)))(),
                    reads=[wkeys[kc], rhs_keys[kc]], writes=[bk(dst_bank)], signal=(i == nk - 1))

        ybank = lambda oc: (oc % 4, oc // 4)

        def ps_square(c, buf, nb, oc=None):
            b, hf = ybank(c if oc is None else oc)
            S.op("act", (lambda c=c, b=b, hf=hf: lambda e: e.activation(out=buf[:, c % nb, :], in_=hb(b, hf),
                                                                        func=AF.Square))(),
                 reads=[], writes=["ysq%d" % (c % nb), bk(b)])

        def ps_mm(c, buf, nb):
            S.op("pe", (lambda c=c: lambda e: e.matmul(hb(7, 1), lhsT=ones[:], rhs=buf[:, c % nb, :],
                                                      start=(c == 0), stop=(c == 7)))(),
                 reads=["ones", "ysq%d" % (c % nb)], writes=[bk(7)], signal=True)

        def ps_finish():
            S.op("act", lambda e: e.activation(out=ms_post[:], in_=hb(7, 1), func=AF.Ln, scale=1.0 / D, bias=epsc[:, 0:1]),
                 reads=["epsc"], writes=["ms_post", bk(7)])
            S.op("act", lambda e: e.activation(out=rstd_post[:], in_=ms_post[:], func=AF.Exp, scale=-0.5),
                 reads=["ms_post"], writes=["rstd_post"])

        def PN(t, norm, tbuf=None, nt=2, last=False):
            S.cur = "PN"
            tb = tmp if tbuf is None else tbuf
            for oc in range(8):
                b, hf = ybank(oc)
                S.op("dve", (lambda oc=oc, b=b, hf=hf: lambda e: e.scalar_tensor_tensor(
                    out=tb[:, oc % nt, :], in0=hb(b, hf), scalar=gcol(norm, oc), in1=rstd_post[:, :],
                    op0=ALU.mult, op1=ALU.mult))(),
                    reads=["small", "rstd_post"], writes=["tmp%d" % (oc % nt), bk(b)])
                xs = lambda oc=oc: x_sb[:, oc, t * TT:(t + 1) * TT]
                S.op("dve" if (last and oc % 2 == 1) else "pool", (lambda oc=oc, xs=xs: lambda e: e.tensor_tensor(
                    out=xs(), in0=xs(), in1=tb[:, oc % nt, :], op=ALU.add))(),
                    reads=["tmp%d" % (oc % nt)], writes=["x%d_%d" % (oc, t)])

        def outproj(w_sb, wkey, rhs_fn, rhs_keys, first_bank=0, korder=None):
            ocs = [((first_bank + i) % 4) + 4 * half for half in range(2) for i in range(4)]
            for idx, oc in enumerate(ocs):
                S.cur = "outproj"
                b, hf = ybank(oc)
                proj(b, hf, TT, w_sb, wkey, oc * 128, rhs_fn, rhs_keys, korder=korder)
                S.cur = "post_stats"
                ps_square(idx, ysq, 2, oc=oc)
                if idx >= 1:
                    ps_mm(idx - 1, ysq, 2)
            ps_mm(7, ysq, 2)
            ps_finish()

        xfn = lambda t: (lambda c: x_sb[:, c, t * TT:(t + 1) * TT])
        xkeys = lambda t: ["x%d_%d" % (c, t) for c in range(8)]
        h0k = ["h0_%d" % c for c in range(8)]

        sp1_keys = ["th0", "th1", "mu", "v1", "rstdB", "nmr", "t10", "t11", "t20", "t21",
                    "ca0", "ca1", "cat0", "cat1"] + \
                   ["zb%d_%d" % (p, c) for p in range(2) for c in range(4)] + ["zbh0", "zbh1", "cvh0", "cvh1"] + \
                   ["cv%d_%d" % (p, c) for p in range(2) for c in range(4)] + \
                   ["ya%d_%d" % (p, c) for p in range(2) for c in range(4)] + ["yb%d" % c for c in range(4)] + \
                   ["zc%d" % c for c in range(4)] + ["zq%d" % c for c in range(4)] + ["sq8_%d" % c for c in range(8)]
        w_in_keys = [p + str(k) for p in ("w_in_cv", "w_in_g", "w_in_b") for k in range(8)]
        regionA1 = w_in_keys + ["w_out%d" % k for k in range(8)] + ["diag%d" % p for p in range(4)]
        sp2_keys = ["qT%d" % c for c in range(8)] + ["p%d" % i for i in range(4)] + ["rs%d" % i for i in range(4)] + \
                   ["oT%d" % c for c in range(8)] + ["memT", "hm", "KT", "V"]
        COL_B, COL_C, COL_V, COL_GV, COL_GG = 0, 512, 1024, 1536, 2048

        def gu_pe(t, j):
            S.cur = "GU"
            hp = t % 2
            hk = ["h%d_%d" % (hp, c) for c in range(8)]
            hfn = lambda kc, hp=hp: hh[hp][:, kc, :]
            b = 4 + j % 3
            proj(b, 0, TT, wg, "wg", j * 128, hfn, hk)
            proj(b, 1, TT, None, None, 0, hfn, hk, lhs_fn=(lambda kc, j=j: wu[:, j, kc, :]),
                 wkeys=["wu%d" % j] * 8)

        def gu_ew(t, j):
            S.cur = "GU"
            b = 4 + j % 3
            S.op("act", (lambda j=j, b=b: lambda e: e.activation(out=sg[:, j % 2, :], in_=hb(b, 0),
                                                                 func=AF.Silu))(),
                 reads=[], writes=["sg%d" % (j % 2), bk(b)])
            S.op("dve", (lambda j=j, b=b: lambda e: e.tensor_tensor(out=actj[:, j % 4, :], in0=hb(b, 1),
                                                                    in1=sg[:, j % 2, :], op=ALU.mult))(),
                 reads=["sg%d" % (j % 2)], writes=["aj%d" % (j % 4), bk(b)])

        def norm_mem():
            S.cur = "KV"
            S.alias(["hm"], ["v1", "rstdB", "nmr", "t10", "t11", "mu"])
            for c in range(8):
                S.op("dve", (lambda c=c: lambda e: e.scalar_tensor_tensor(
                    out=hm[:, c, :], in0=memT[:, c, :], scalar=sm[:, SM_MEMG + c:SM_MEMG + c + 1],
                    in1=rstd_pre[:, 0:NMEM], op0=ALU.mult, op1=ALU.mult))(),
                    reads=["memT", "small", "rstd_pre"], writes=["hm"])

        def run_sp1():
            rr = [0]

            def fbank():
                rr[0] = (rr[0] + 1) % 4
                return rr[0]

            def F(n, par, dcol, halo_only, cs=(0, 1, 2, 3), parts=("cv", "glu", "a")):
                S.cur = "F"
                hfn = (lambda kc: h_halo[:, kc, 0:n]) if halo_only else (lambda kc: hh[0][:, kc, 0:n])
                h0k = ["hh_%d" % c for c in range(8)] if halo_only else ["h0_%d" % c for c in range(8)]
                zkey = (lambda c: "zbh%d" % par) if halo_only else (lambda c: "zb%d_%d" % (par, c))
                ckey = (lambda c: "cvh%d" % par) if halo_only else (lambda c: "cv%d_%d" % (par, c))
                for c in cs:
                    if "cv" in parts:
                        b = fbank()
                        proj(b, 0, n, w_in, "w_in_cv", COL_C + c * 128, hfn, h0k)
                        proj(b, 1, n, w_in, "w_in_cv", COL_V + c * 128, hfn, h0k)
                        S.op("act", (lambda c=c, b=b: lambda e: e.activation(out=ca[:, c % 2, 0:n], in_=hb(b, 0, n),
                                                                             func=AF.Copy))(),
                             reads=[], writes=["ca%d" % (c % 2), bk(b)])
                        S.op("dve", (lambda c=c, b=b: lambda e: e.tensor_tensor(
                            out=cvbuf[:, par, c, dcol:dcol + n], in0=hb(b, 1, n), in1=ca[:, c % 2, 0:n], op=ALU.mult))(),
                            reads=["ca%d" % (c % 2)], writes=[ckey(c), bk(b)])
                    if "a" in parts and not halo_only:
                        ckeys = ["cv%d_%d" % (par, c), "cvh%d" % par, "small"]
                        S.op("dve", (lambda c=c: lambda e: e.tensor_scalar(
                            out=cat[:, c % 2, :], in0=cvbuf[:, par, c, 30:30 + TT],
                            scalar1=sm[:, SM_CAW + c:SM_CAW + c + 1], scalar2=None, op0=ALU.mult))(),
                            reads=ckeys, writes=["cat%d" % (c % 2)])
                        for k in (1, 2):
                            S.op("dve", (lambda c=c, k=k: lambda e: e.scalar_tensor_tensor(
                                out=cat[:, c % 2, :], in0=cvbuf[:, par, c, 30 + k:30 + k + TT],
                                scalar=sm[:, SM_CAW + k * 4 + c:SM_CAW + k * 4 + c + 1], in1=cat[:, c % 2, :],
                                op0=ALU.mult, op1=ALU.add))(),
                                reads=ckeys + ["cat%d" % (c % 2)], writes=["cat%d" % (c % 2)])
                        b = fbank()
                        proj(b, 0, n, w_in, "w_in_b", COL_B + c * 128, hfn, h0k)
                        S.op("dve", (lambda c=c, b=b: lambda e: e.tensor_tensor(
                            out=ycat_a[:, par, c, :], in0=hb(b, 0), in1=cat[:, c % 2, :], op=ALU.mult))(),
                            reads=["cat%d" % (c % 2)], writes=["ya%d_%d" % (par, c), bk(b)])
                    if "glu" in parts:
                        b = fbank()
                        proj(b, 0, n, w_in, "w_in_g", COL_GG + c * 128, hfn, h0k)
                        proj(b, 1, n, w_in, "w_in_g", COL_GV + c * 128, hfn, h0k)
                        S.op("act", (lambda c=c, b=b: lambda e: e.activation(out=th[:, c % 2, 0:n], in_=hb(b, 0, n),
                                                                             func=AF.Tanh, scale=0.5))(),
                             reads=[], writes=["th%d" % (c % 2), bk(b)])
                        S.op("dve", (lambda c=c, b=b: lambda e: e.scalar_tensor_tensor(
                            out=zbuf[:, par, c, dcol:dcol + n], in0=th[:, c % 2, 0:n], scalar=1.0, in1=hb(b, 1, n),
                            op0=ALU.add, op1=ALU.mult))(),
                            reads=["th%d" % (c % 2)], writes=[zkey(c), bk(b)])

            def HC(par):
                S.op("pool", lambda e, par=par: e.tensor_copy(out=zbuf[:, 1 - par, :, 0:HALO],
                                                              in_=zbuf[:, par, :, TT:TT + HALO]),
                     reads=["zb%d_%d" % (par, c) for c in range(4)], writes=["zbh%d" % (1 - par)])
                S.op("pool", lambda e, par=par: e.tensor_copy(out=cvbuf[:, 1 - par, :, 0:HALO],
                                                              in_=cvbuf[:, par, :, TT:TT + HALO]),
                     reads=["cv%d_%d" % (par, c) for c in range(4)], writes=["cvh%d" % (1 - par)])

            def B1(par, chunks):
                S.cur = "B1"
                for c in chunks:
                    b, hf = 4 + c % 2, c // 2
                    for k in range(31):
                        di = 12 + k * 4 + c
                        S.op("pe", (lambda c=c, k=k, di=di, b=b, hf=hf, par=par: lambda e: e.matmul(
                            hb(b, hf), lhsT=diag[:, di, :], rhs=zbuf[:, par, c, 2 + k:2 + k + TT],
                            start=(k == 0), stop=(k == 30)))(),
                            reads=["diag%d" % (di // 34), "zb%d_%d" % (par, c), "zbh%d" % par], writes=[bk(b)],
                            signal=(k == 30))
                    S.op("act", (lambda c=c, b=b, hf=hf: lambda e: e.activation(
                        out=zc[:, c, :], in_=hb(b, hf), func=AF.Identity, bias=sm[:, SM_CBB + c:SM_CBB + c + 1]))(),
                        reads=["small"], writes=["zc%d" % c, bk(b)])
                    S.op("act", (lambda c=c, b=b, hf=hf: lambda e: e.activation(
                        out=zq[:, c, :], in_=hb(b, hf), func=AF.Square, bias=sm[:, SM_CBB + c:SM_CBB + c + 1]))(),
                        reads=["small"], writes=["zq%d" % c, bk(b)])

            def B2_stats():
                S.cur = "B2"
                for c in range(4):
                    S.op("pe", (lambda c=c: lambda e: e.matmul(hb(6, 0), lhsT=ones[:], rhs=zc[:, c, :],
                                                              start=(c == 0), stop=(c == 3)))(),
                         reads=["ones", "zc%d" % c], writes=[bk(6)], signal=(c == 3))
                for c in range(4):
                    S.op("pe", (lambda c=c: lambda e: e.matmul(hb(6, 1), lhsT=ones[:], rhs=zq[:, c, :],
                                                              start=(c == 0), stop=(c == 3)))(),
                         reads=["ones", "zq%d" % c], writes=[bk(6)], signal=(c == 3))
                S.op("dve", lambda e: e.tensor_scalar(out=mu[:], in0=hb(6, 0), scalar1=1.0 / 512, scalar2=None,
                                                      op0=ALU.mult),
                     writes=["mu", bk(6)])
                S.op("dve", lambda e: e.tensor_tensor(out=v1[:], in0=mu[:], in1=mu[:], op=ALU.mult),
                     reads=["mu"], writes=["v1"])
                S.op("dve", lambda e: e.scalar_tensor_tensor(out=v1[:], in0=hb(6, 1), scalar=1.0 / 512, in1=v1[:],
                                                             op0=ALU.mult, op1=ALU.subtract),
                     reads=["v1"], writes=["v1", bk(6)])
                S.op("act", lambda e: e.activation(out=v1[:], in_=v1[:], func=AF.Ln, bias=epsc[:, 1:2]),
                     reads=["v1", "epsc"], writes=["v1"])
                S.op("act", lambda e: e.activation(out=rstdB[:], in_=v1[:], func=AF.Exp, scale=-0.5),
                     reads=["v1"], writes=["rstdB"])
                S.op("dve", lambda e: e.scalar_tensor_tensor(out=nmr[:], in0=mu[:], scalar=-1.0, in1=rstdB[:],
                                                             op0=ALU.mult, op1=ALU.mult),
                     reads=["mu", "rstdB"], writes=["nmr"])

            def B2_chunk(c):
                S.cur = "B2"
                S.op("dve", (lambda c=c: lambda e: e.tensor_tensor(out=t1[:, c % 2, :], in0=zc[:, c, :], in1=rstdB[:],
                                                                   op=ALU.mult))(),
                     reads=["zc%d" % c, "rstdB"], writes=["t1%d" % (c % 2)])
                S.op("dve", (lambda c=c: lambda e: e.tensor_tensor(out=t2[:, c % 2, :], in0=t1[:, c % 2, :],
                                                                   in1=nmr[:], op=ALU.add))(),
                     reads=["t1%d" % (c % 2), "nmr"], writes=["t2%d" % (c % 2)])
                S.op("act", (lambda c=c: lambda e: e.activation(
                    out=ycat_b[:, c, :], in_=t2[:, c % 2, :], func=AF.Silu,
                    scale=sm[:, SM_LNG + c:SM_LNG + c + 1], bias=sm[:, SM_LNB + c:SM_LNB + c + 1]))(),
                    reads=["t2%d" % (c % 2), "small"], writes=["yb%d" % c])

            def O(t, par):
                yfn = lambda kc: ycat_a[:, par, kc, :] if kc < 4 else ycat_b[:, kc - 4, :]
                yk = ["ya%d_%d" % (par, c) for c in range(4)] + ["yb%d" % c for c in range(4)]
                outproj(w_out, "w_out", yfn, yk, first_bank=(rr[0] + 1) % 4, korder=(4, 5, 6, 7, 0, 1, 2, 3))
                PN(t, 1)

            hx = lambda c: xh_sb[:, c, :]
            PA(hx, ["xh"] * 8, HALO, sq8_1)
            PB(HALO, sq8_1)
            PA(xfn(0), xkeys(0), TT, sq8_1)
            PC(hx, ["xh"] * 8, HALO, 0, halo=True)
            PB(TT, sq8_1)
            PC(xfn(0), xkeys(0), TT, 0)
            F(HALO, 0, 0, True, parts=("cv",))
            F(TT, 0, HALO, False, parts=("cv",))
            F(HALO, 0, 0, True, parts=("glu",))
            F(TT, 0, HALO, False, parts=("glu",))
            F(TT, 0, HALO, False, parts=("a",))
            S.alias(["ysq0", "ysq1"], ["hh_%d" % c for c in range(8)])
            for t in range(NT):
                par = t % 2
                nxt = t + 1 < NT
                if nxt:
                    HC(par)
                    PA(xfn(t + 1), xkeys(t + 1), TT, sq8_1)
                elif NPHASES >= 2:
                    PA(xfn(0), xkeys(0), TT, sq8_1)
                B1(par, [0, 1] if t == 0 else [1, 2])
                if nxt:
                    PB(TT, sq8_1)
                    PC(xfn(t + 1), xkeys(t + 1), TT, 0)
                elif NPHASES >= 2:
                    PB(TT, sq8_1)
                    PC(xfn(0), xkeys(0), TT, 2)
                    PA(lambda c: memT[:, c, :], ["memT"] * 8, NMEM, sq8_1)
                B1(par, [2, 3] if t == 0 else [3])
                if not nxt and NPHASES >= 2:
                    PB(NMEM, sq8_1)
                    S.alias(["wq%d" % k for k in range(8)] + ["wo%d" % k for k in range(8)],
                            ["diag%d" % p for p in range(4)])
                    load_w(wq, wq_d, "wq")
                    load_w(wo, wo_d, "wo")
                B2_stats()
                if nxt:
                    F(TT, 1 - par, HALO, False, cs=(0,))
                    B2_chunk(0)
                    F(TT, 1 - par, HALO, False, cs=(1,))
                    B2_chunk(1)
                    F(TT, 1 - par, HALO, False, cs=(2,))
                    B2_chunk(2)
                    B2_chunk(3)
                    F(TT, 1 - par, HALO, False, cs=(3,))
                    B1(1 - par, [0])
                    if t + 2 == NT and NPHASES >= 2:
                        S.alias(["wk%d" % k for k in range(8)] + ["wv%d" % k for k in range(8)], w_in_keys)
                        load_w(wk, wk_d, "wk")
                        load_w(wv, wv_d, "wv")
                        S.alias(["memT"], w_in_keys)
                        S.dma("sp", "memT", lambda e: e.dma_start(out=memT[:], in_=memT_d), writes=["memT"])
                else:
                    for c in range(4):
                        B2_chunk(c)
                    if NPHASES >= 2:
                        norm_mem()
                O(t, par)

        def run_sp2():
            S.alias(sp2_keys + ["sq8_%d" % c for c in range(8)], sp1_keys)
            rr = [0]

            def sbank():
                rr[0] = (rr[0] + 1) % 3
                return 4 + rr[0]

            hfn = lambda kc: hh[0][:, kc, :]

            def Q(e2s=(0, 1, 2, 3)):
                S.cur = "Q"
                for e2 in e2s:
                    b = sbank()
                    for hf in range(2):
                        ec = e2 * 2 + hf
                        proj(b, hf, TT, wq, "wq", ec * 128, hfn, h0k)
                    S.op("act", (lambda e2=e2, b=b: lambda e: e.activation(
                        out=qT[:, 2 * e2:2 * e2 + 2, :], in_=ps[b][:, :].rearrange("p (a n) -> p a n", a=2),
                        func=AF.Copy))(),
                        reads=[], writes=["qT%d" % (2 * e2), "qT%d" % (2 * e2 + 1), bk(b)])

            Q()
            S.cur = "KV"
            for e2 in range(4):
                b = sbank()
                for hf in range(2):
                    ec = e2 * 2 + hf
                    proj(b, hf, NMEM, wk, "wk", ec * 128, lambda kc: hm[:, kc, :], ["hm"] * 8)
                S.op("act", (lambda e2=e2, b=b: lambda e: e.activation(
                    out=KT[:, 2 * e2:2 * e2 + 2, :], in_=ps[b][:, :].rearrange("p (a n) -> p a n", a=2), func=AF.Copy))(),
                    reads=[], writes=["KT", bk(b)])
            for mc in range(2):
                for eh in range(2):
                    b = sbank()
                    for kc in range(8):
                        S.op("pe", (lambda kc=kc, mc=mc, eh=eh, b=b: lambda e: e.matmul(
                            ps[b][:, :], lhsT=hm[:, kc, mc * 128:(mc + 1) * 128], rhs=wv[:, kc, eh * 512:(eh + 1) * 512],
                            start=(kc == 0), stop=(kc == 7)))(),
                            reads=["hm", "wv%d" % kc], writes=[bk(b)], signal=(kc == 7))
                    S.op("act", (lambda mc=mc, eh=eh, b=b: lambda e: e.activation(
                        out=Vt[:, mc, eh * 512:(eh + 1) * 512], in_=ps[b][:, :], func=AF.Copy))(),
                        reads=[], writes=["V", bk(b)])
            S.alias(["qT%d" % c for c in range(8)] + ["p%d" % i for i in range(4)] + ["rs%d" % i for i in range(4)],
                    ["memT", "hm"])
            S.alias(["wg%d" % k for k in range(8)],
                    ["wk%d" % k for k in range(8)] + ["wv%d" % k for k in range(8)] +
                    w_in_keys + ["w_out%d" % k for k in range(8)])
            load_w(wg, wg_d, "wg")
            S.alias(["wu%d" % j for j in range(0, 7)], ["w_out%d" % k for k in range(8)] + ["diag%d" % p for p in range(4)])
            load_wu(0, 7, "wu_a")

            def SC():
                S.cur = "SC"
                for hd in range(4):
                    b = sbank()
                    for mc in range(2):
                        for j in range(2):
                            ec = 2 * hd + j
                            S.op("pe", (lambda mc=mc, j=j, ec=ec, b=b: lambda e: e.matmul(
                                hb(b, mc), lhsT=KT[:, ec, mc * 128:(mc + 1) * 128], rhs=qT[:, ec, :],
                                start=(j == 0), stop=(j == 1)))(),
                                reads=["KT", "qT%d" % ec], writes=[bk(b)], signal=(j == 1))
                    S.op("act", (lambda hd=hd, b=b: lambda e: e.activation(
                        out=pbuf[:, hd, :, :], in_=ps[b][:, :].rearrange("p (a n) -> p a n", a=2), func=AF.Exp,
                        scale=1.0 / 16.0))(),
                        reads=[], writes=["p%d" % hd, bk(b)])

            def SM_PV(sb0=4):
                S.cur = "SM_PV"
                for hd in range(4):
                    for mc in range(2):
                        S.op("pe", (lambda mc=mc, hd=hd: lambda e: e.matmul(
                            hb(sb0 + hd // 2, hd % 2), lhsT=ones[:], rhs=pbuf[:, hd, mc, :], start=(mc == 0), stop=(mc == 1)))(),
                            reads=["ones", "p%d" % hd], writes=[bk(sb0 + hd // 2)], signal=(mc == 1))
                for hd in range(4):
                    for dc in range(2):
                        ec = 2 * hd + dc
                        for mc in range(2):
                            S.op("pe", (lambda mc=mc, ec=ec, dc=dc, hd=hd: lambda e: e.matmul(
                                hb(hd, dc), lhsT=Vt[:, mc, ec * 128:(ec + 1) * 128], rhs=pbuf[:, hd, mc, :],
                                start=(mc == 0), stop=(mc == 1)))(),
                                reads=["V", "p%d" % hd], writes=[bk(hd)], signal=(mc == 1))
                for h2 in range(2):
                    S.op("act", (lambda h2=h2: lambda e: e.activation(
                        out=rs[:, 2 * h2:2 * h2 + 2, :], in_=ps[sb0 + h2][:, :].rearrange("p (a n) -> p a n", a=2),
                        func=AF.Ln))(),
                        reads=[], writes=["rs%d" % (2 * h2), "rs%d" % (2 * h2 + 1), bk(sb0 + h2)])
                    S.op("act", (lambda h2=h2: lambda e: e.activation(
                        out=rs[:, 2 * h2:2 * h2 + 2, :], in_=rs[:, 2 * h2:2 * h2 + 2, :], func=AF.Exp, scale=-1.0))(),
                        reads=["rs%d" % (2 * h2), "rs%d" % (2 * h2 + 1)], writes=["rs%d" % (2 * h2), "rs%d" % (2 * h2 + 1)])
                for hd in range(4):
                    for dc in range(2):
                        ec = 2 * hd + dc
                        S.op("dve", (lambda ec=ec, dc=dc, hd=hd: lambda e: e.tensor_tensor(
                            out=oT[:, ec, :], in0=hb(hd, dc), in1=rs[:, hd, :], op=ALU.mult))(),
                            reads=["rs%d" % hd], writes=["oT%d" % ec, bk(hd)])

            def O(t):
                outproj(wo, "wo", lambda kc: oT[:, kc, :], ["oT%d" % c for c in range(8)])
                PN(t, 3)

            def PRE(t):
                PA(xfn(t), xkeys(t), TT, sq8_2)
                PB(TT, sq8_2)
                PC(xfn(t), xkeys(t), TT, 2)

            SC()
            if NT > 1:
                PRE(1)
            for t in range(NT):
                if t + 2 < NT:
                    PA(xfn(t + 2), xkeys(t + 2), TT, sq8_2)
                elif t + 2 == NT and NPHASES >= 3:
                    PA(xfn(0), xkeys(0), TT, sq8_2)
                    S.alias(["wdj%d" % i for i in range(4)], ["memT"] + sp1_keys)
                    for g in range(4):
                        load_wd(g)
                if t + 1 < NT:
                    Q()
                    if t + 2 == NT and NPHASES >= 3:
                        S.alias(["wu%d" % j for j in range(7, 15)], ["wq%d" % k for k in range(8)])
                        load_wu(7, 15, "wu_b")
                if t + 1 == NT and NPHASES >= 3:
                    gu_pe(0, 0)
                    gu_pe(0, 1)
                    SM_PV(sb0=6)
                else:
                    SM_PV()
                if t + 1 < NT:
                    SC()
                if t + 2 < NT:
                    PB(TT, sq8_2)
                    PC(xfn(t + 2), xkeys(t + 2), TT, 2)
                elif t + 2 == NT and NPHASES >= 3:
                    PB(TT, sq8_2)
                    PC(xfn(0), xkeys(0), TT, 4, hp=0)
                O(t)

        def run_sp3():
            sp3_keys = ["sg0", "sg1"] + ["aj%d" % i for i in range(4)] + ["wdj%d" % i for i in range(NS)] + \
                       ["h1_%d" % c for c in range(8)] + ["sq8_%d" % c for c in range(8)] + \
                       ["ysq%d" % c for c in range(8)] + ["tmp%d" % c for c in range(4)]
            S.alias(sp3_keys, sp2_keys + sp1_keys)
            NG = NT * NJ

            for g in range(4, NS):
                load_wd(g)
            S.alias(["wu%d" % j for j in range(15, NJ)], ["wo%d" % k for k in range(8)])
            load_wu(15, NJ, "wu_c")

            def down(t, j):
                S.cur = "down"
                g = t * NJ + j
                slot = g % NS
                for oc in range(8):
                    b, hf = ybank(oc)
                    S.op("pe", (lambda oc=oc, b=b, hf=hf, slot=slot, j=j: lambda e: e.matmul(
                        hb(b, hf), lhsT=wd_slot_cols(slot, oc * 128, (oc + 1) * 128), rhs=actj[:, j % 4, :],
                        start=(j == 0 and oc < 4), stop=(j == NJ - 1), skip_group_check=True))(),
                        reads=["wdj%d" % slot, "aj%d" % (j % 4)], writes=[bk(b)], signal=(oc == 7))
                if g + NS < NG:
                    load_wd(g + NS)

            def PRE_A(t):
                PA(xfn(t), xkeys(t), TT, sq8_3)

            def PRE_B(t):
                PB(TT, sq8_3)

            def PRE_C(t):
                PC(xfn(t), xkeys(t), TT, 4, hp=t % 2)

            out_toks = []
            LAG = 3
            if NT > 1:
                PRE_A(1)

            def finish_tile(t):
                S.cur = "post_stats"
                for i, c in enumerate(reversed(range(8))):
                    S.op("pe", (lambda c=c, i=i: lambda e: e.matmul(hb(7, 1), lhsT=ones[:], rhs=ysq8[:, c, :],
                                                                   start=(i == 0), stop=(i == 7)))(),
                         reads=["ones", "ysq%d" % c], writes=[bk(7)], signal=(i == 7))
                ps_finish()
                last = (t + 1 == NT)
                PN(t, 5, tbuf=tmp4, nt=4, last=last)
                if not last:
                    out_toks.append(S.dma("sp", "out",
                                          (lambda t=t: lambda e: e.dma_start(out=outT_d[:, :, t * TT:(t + 1) * TT],
                                                                             in_=x_sb[:, :, t * TT:(t + 1) * TT]))(),
                                          reads=["x%d_%d" % (c, t) for c in range(8)]))
                else:
                    for c2 in range(4):
                        out_toks.append(S.dma("sp", "out",
                                              (lambda t=t, c2=c2: lambda e: e.dma_start(
                                                  out=outT_d[:, 2 * c2:2 * c2 + 2, t * TT:(t + 1) * TT],
                                                  in_=x_sb[:, 2 * c2:2 * c2 + 2, t * TT:(t + 1) * TT]))(),
                                              reads=["x%d_%d" % (c, t) for c in (2 * c2, 2 * c2 + 1)]))

            for t in range(NT):
                hp = t % 2
                hk = ["h%d_%d" % (hp, c) for c in range(8)]
                hfn = lambda kc, hp=hp: hh[hp][:, kc, :]
                for j in range(NJ):
                    if j == 1 and t >= 1:
                        finish_tile(t - 1)
                    if not (t == 0 and j < 2):
                        gu_pe(t, j)
                    gu_ew(t, j)
                    if j == LAG and t + 1 < NT:
                        PRE_B(t + 1)
                        S.cur = "down"
                    if j >= LAG:
                        down(t, j - LAG)
                    if j == 10 and t + 2 < NT:
                        PRE_A(t + 2)
                    elif j == 12 and t + 1 < NT:
                        PRE_C(t + 1)
                for jj in range(NJ - LAG, NJ):
                    down(t, jj)
                S.cur = "post_stats"
                if t + 1 == NT:
                    S.op("act", lambda e: e.activation(out=ms_post[:, 0:1], in_=epsc[:, 0:1], func=AF.Ln),
                         reads=["epsc"], writes=["ms_post"])
                for c in range(8):
                    ps_square(c, ysq8, 8)
                if t + 1 == NT:
                    finish_tile(t)
            return out_toks

        out_toks = []
        if NPHASES >= 1:
            run_sp1()
        if NPHASES >= 2:
            run_sp2()
        if NPHASES >= 3:
            out_toks = run_sp3()
        else:
            for t in range(NT):
                out_toks.append(S.dma("sp", "out",
                                      (lambda t=t: lambda e: e.dma_start(out=outT_d[:, :, t * TT:(t + 1) * TT],
                                                                         in_=x_sb[:, :, t * TT:(t + 1) * TT]))(),
                                      reads=["x%d_%d" % (c, t) for c in range(8)]))
        S.wait_all("sp", out_toks)

        with nc.Block() as block:
            S.emit(block)
    return nc


_PROGRAM = None
_LAST_SCHED = None


def kernel(x, mem, mix_pre_g, w_mix_in, conv_a_w, conv_b_w, conv_b_b, ln_b_g, ln_b_b,
           w_mix_out, mix_post_g, xa_pre_g, mem_norm_g, w_q, w_k, w_v, w_o, xa_post_g,
           ffn_pre_g, w_gate, w_up, w_down, ffn_post_g):
    global _PROGRAM
    f = lambda a: np.ascontiguousarray(np.asarray(a, dtype=np.float32))
    x = f(x).reshape(SEQ, D)
    xp = np.concatenate([np.zeros((HALO, D), np.float32), x], axis=0)
    in_maps = []
    vec8 = lambda v: f(v).reshape(8, 128).T
    vec4 = lambda v: f(v).reshape(4, 128).T
    small = np.zeros((128, SM_N), np.float32)
    for i, g in enumerate([mix_pre_g, mix_post_g, xa_pre_g, xa_post_g, ffn_pre_g, ffn_post_g]):
        small[:, SM_G + i * 8:SM_G + (i + 1) * 8] = vec8(g)
    small[:, SM_MEMG:SM_MEMG + 8] = vec8(mem_norm_g)
    small[:, SM_CBB:SM_CBB + 4] = vec4(conv_b_b)
    small[:, SM_LNG:SM_LNG + 4] = vec4(ln_b_g)
    small[:, SM_LNB:SM_LNB + 4] = vec4(ln_b_b)
    caw = f(conv_a_w)
    cbw = f(conv_b_w)
    for k in range(3):
        small[:, SM_CAW + k * 4:SM_CAW + (k + 1) * 4] = vec4(caw[k])
    for k in range(31):
        small[:, SM_CBW + k * 4:SM_CBW + (k + 1) * 4] = vec4(cbw[k])
    ident = np.eye(128, dtype=np.float32)
    memT = np.ascontiguousarray(f(mem).reshape(NMEM, 8, 128).transpose(2, 1, 0))
    shared = {
        "small": small, "ident": ident, "memT": memT,
        "w_mix_in": f(w_mix_in), "w_mix_out": f(w_mix_out), "w_q": f(w_q), "w_k": f(w_k), "w_v": f(w_v),
        "w_o": f(w_o), "w_gate": f(w_gate), "w_up": f(w_up), "w_down": f(w_down),
    }
    for c in range(NCORES):
        seg = xp[c * TPC:c * TPC + HALO + TPC]
        xT = np.ascontiguousarray(seg.reshape(HALO + TPC, 8, 128).transpose(2, 1, 0))
        m = dict(shared)
        m["xT"] = xT
        in_maps.append(m)
    if _PROGRAM is None:
        _PROGRAM = build_program()
    res = run_bass_kernel_spmd(_PROGRAM, in_maps, core_ids=list(range(NCORES)))
    out = np.empty((SEQ, D), np.float32)
    for c in range(NCORES):
        oT = np.asarray(res.results[c]["outT"]).reshape(128, 8, TPC)
        out[c * TPC:(c + 1) * TPC] = oT.transpose(2, 1, 0).reshape(TPC, D)
    return out.reshape(1, SEQ, D)
```

```python
from contextlib import ExitStack
import numpy as np
import concourse.bass as bass
import concourse.mybir as mybir
from concourse.bass_utils import run_bass_kernel_spmd

F32 = mybir.dt.float32
BF16 = mybir.dt.bfloat16
ALU = mybir.AluOpType
AF = mybir.ActivationFunctionType

NCORES = 8
D = 1024
SEQ = 16384
TPC = SEQ // NCORES
TT = 256
NT = TPC // TT
HALO = 32
NMEM = 256
DFF = 2816
NJ = DFF // 128
DIN = 2560
RMS_EPS = 1e-6
LN_EPS = 1e-5

SM_G = 0
SM_MEMG = 48
SM_CBB = 56
SM_LNG = 60
SM_LNB = 64
SM_CAW = 68
SM_CBW = 80
SM_N = 204

ENGS = ["pe", "act", "dve", "pool", "sp"]
NPHASES = 3
SB_BASE = 17408


class Sched:
    def __init__(self, nc, stack):
        self.nc = nc
        self.stack = stack
        self.lists = {e: [] for e in ENGS}
        self.count = {e: 0 for e in ENGS}
        self.waited = {e: {} for e in ENGS}
        self.sems = {}
        self.dma_cum = {}
        self.bufs = {}
        self.known = {e: {} for e in ENGS}
        self.snap = {}
        self.seq = {}
        self.nrec = 0
        self.cur = ""
        for e in ENGS:
            self.sems[e] = stack.enter_context(nc.semaphore("s_" + e))

    def dma_sem(self, name):
        key = "dma_" + name
        if key not in self.sems:
            self.sems[key] = self.stack.enter_context(self.nc.semaphore(key))
            self.dma_cum[key] = 0
        return key

    def _deps(self, eng, reads, writes):
        toks = []
        for k in reads:
            st = self.bufs.get(k)
            if st and st["w"] is not None:
                toks.append(st["w"])
        for k in writes:
            st = self.bufs.get(k)
            if st:
                if st["w"] is not None:
                    toks.append(st["w"])
                toks.extend(st["r"].items())
        need = {}
        for (sk, v) in toks:
            if sk == "pe" and eng == "pe":
                continue
            if v > need.get(sk, 0):
                need[sk] = v
        waits = []
        kn = self.known[eng]
        for sk, v in sorted(need.items(), key=lambda it: -self.seq.get(it, 0)):
            if kn.get(sk, 0) >= v:
                continue
            waits.append((sk, v))
            kn[sk] = v
            sn = self.snap.get((sk, v))
            if sn:
                for k2, v2 in sn.items():
                    if v2 > kn.get(k2, 0):
                        kn[k2] = v2
        return waits

    def _update(self, tok, reads, writes):
        for k in reads:
            st = self.bufs.setdefault(k, {"w": None, "r": {}})
            if tok[1] > st["r"].get(tok[0], 0):
                st["r"][tok[0]] = tok[1]
        for k in writes:
            self.bufs[k] = {"w": tok, "r": {}}

    def alias(self, newkeys, oldkeys):
        merged = {}
        for k in oldkeys:
            st = self.bufs.get(k)
            if not st:
                continue
            items = list(st["r"].items())
            if st["w"] is not None:
                items.append(st["w"])
            for sk, v in items:
                if v > merged.get(sk, 0):
                    merged[sk] = v
        for k in newkeys:
            st = self.bufs.setdefault(k, {"w": None, "r": {}})
            for sk, v in merged.items():
                if v > st["r"].get(sk, 0):
                    st["r"][sk] = v

    def op(self, eng, fn, reads=(), writes=(), signal=True):
        waits = self._deps(eng, reads, writes)
        if signal:
            self.count[eng] += 1
            tok = (eng, self.count[eng])
            inc = (eng, 1)
            sn = dict(self.known[eng])
            sn.pop(eng, None)
            self.snap[tok] = sn
            self.nrec += 1
            self.seq[tok] = self.nrec
        else:
            tok = (eng, self.count[eng] + 1)
            inc = None
        self.lists[eng].append((waits, fn, inc, self.cur))
        self._update(tok, reads, writes)
        return tok

    def dma(self, eng, semname, fn, reads=(), writes=()):
        key = self.dma_sem(semname)
        waits = self._deps(eng, reads, writes)
        self.dma_cum[key] += 16
        tok = (key, self.dma_cum[key])
        sn = dict(self.known[eng])
        sn.pop(eng, None)
        self.snap[tok] = sn
        self.nrec += 1
        self.seq[tok] = self.nrec
        self.lists[eng].append((waits, fn, (key, 16), self.cur))
        self._update(tok, reads, writes)
        return tok

    def wait_all(self, eng, toks):
        need = {}
        for (sk, v) in toks:
            if v > need.get(sk, 0):
                need[sk] = v
        self.lists[eng].append((list(need.items()), None, None, self.cur))

    def emit(self, block):
        sems = self.sems
        lists = self.lists

        def run(engname):
            def body(e):
                for (waits, fn, inc, _lbl) in lists[engname]:
                    for (sk, v) in waits:
                        e.wait_ge(sems[sk], v)
                    if fn is not None:
                        ins = fn(e)
                        if inc is not None:
                            ins.then_inc(sems[inc[0]], inc[1])
            return body

        block.tensor(run("pe"))
        block.scalar(run("act"))
        block.vector(run("dve"))
        block.gpsimd(run("pool"))
        block.sync(run("sp"))


def build_program():
    nc = bass.Bass("TRN2", target_bir_lowering=False)
    dr = lambda name, shape, kind="ExternalInput": nc.dram_tensor(name, shape, F32, kind=kind).ap()
    xT_d = dr("xT", [128, 8, HALO + TPC])
    small_d = dr("small", [128, SM_N])
    ident_d = dr("ident", [128, 128])
    memT_d = dr("memT", [128, 8, NMEM])
    w_in_d = dr("w_mix_in", [D, DIN])
    w_out_d = dr("w_mix_out", [D, D])
    wq_d = dr("w_q", [D, D])
    wk_d = dr("w_k", [D, D])
    wv_d = dr("w_v", [D, D])
    wo_d = dr("w_o", [D, D])
    wg_d = dr("w_gate", [D, DFF])
    wu_d = dr("w_up", [D, DFF])
    wd_d = dr("w_down", [DFF, D])
    outT_d = dr("outT", [128, 8, TPC], kind="ExternalOutput")
    wd_bf = nc.dram_tensor("wd_bf16", [DFF, D], BF16).ap()

    with ExitStack() as stack:
        S = Sched(nc, stack)
        global _LAST_SCHED
        _LAST_SCHED = S
        _n = [0]

        def sbt(shape, dt, off):
            _n[0] += 1
            nbytes = int(np.prod(shape[1:])) * (4 if dt == F32 else 2)
            assert SB_BASE + off + nbytes <= 229344, (shape, off)
            return nc.alloc_sbuf_tensor_at("t%d" % _n[0], list(shape), dt, offset=SB_BASE + off)

        X0 = 0
        XH0 = 65536
        SM0 = 66560
        ID0 = 67392
        ON0 = 67904
        NH0 = 68160
        KT0 = 69184
        V0 = 73280
        A0 = 77376
        W0 = 169536
        WSIZE = 229344 - SB_BASE - W0
        x_sb = sbt([128, 8, TPC], F32, X0)
        xh_sb = sbt([128, 8, HALO], F32, XH0)
        sm = sbt([128, SM_N], F32, SM0)
        ident = sbt([128, 128], F32, ID0)
        ones = sbt([128, 128], BF16, ON0)
        epsc = sbt([128, 2], F32, NH0)
        KT = sbt([128, 8, NMEM], BF16, KT0)
        Vt = sbt([128, 2, D], BF16, V0)
        w_in = sbt([128, 8, DIN], BF16, A0)
        w_out = sbt([128, 8, D], BF16, A0 + 40960)
        diag = sbt([128, 136, 128], BF16, A0 + 57344)
        wk = sbt([128, 8, D], BF16, A0)
        wv = sbt([128, 8, D], BF16, A0 + 16384)
        wq = sbt([128, 8, D], BF16, A0 + 59392)
        wo = sbt([128, 8, D], BF16, A0 + 75776)
        wg = sbt([128, 8, DFF], BF16, A0)
        wu = sbt([128, NJ, 8, 128], BF16, A0 + 45056)
        ms_pre = sbt([128, TT], F32, W0)
        rstd_pre = sbt([128, TT], F32, W0 + 1024)
        ms_post = sbt([128, TT], F32, W0 + 2048)
        rstd_post = sbt([128, TT], F32, W0 + 3072)
        hh = [sbt([128, 8, TT], BF16, W0 + 4096), sbt([128, 8, TT], BF16, W0 + 11264)]
        ysq = sbt([128, 2, TT], BF16, W0 + 8192)
        h_halo = sbt([128, 8, HALO], BF16, W0 + 8192)
        tmp = sbt([128, 2, TT], F32, W0 + 9216)
        WP = W0 + 11264
        WP3 = W0 + 15360
        sq8_1 = sbt([128, 8, TT], BF16, KT0)
        ycat_a = sbt([128, 2, 4, TT], BF16, V0)
        th = sbt([128, 2, TT], F32, WP)
        zbuf = sbt([128, 2, 4, HALO + TT], BF16, WP + 2048)
        zc = sbt([128, 4, TT], BF16, WP + 6656)
        zq = sbt([128, 4, TT], BF16, WP + 8704)
        mu = sbt([128, TT], F32, WP + 10752)
        v1 = sbt([128, TT], F32, WP + 11776)
        rstdB = sbt([128, TT], F32, WP + 12800)
        nmr = sbt([128, TT], F32, WP + 13824)
        t1 = sbt([128, 2, TT], F32, WP + 14848)
        t2 = sbt([128, 2, TT], F32, WP + 16896)
        ca = sbt([128, 2, TT], F32, WP + 18944)
        cat = sbt([128, 2, TT], F32, WP + 20992)
        cvbuf = sbt([128, 2, 4, HALO + TT], BF16, WP + 23040)
        ycat_b = sbt([128, 4, TT], BF16, WP + 27648)
        assert WP + 29696 - W0 <= WSIZE
        sq8_2 = sbt([128, 8, TT], BF16, WP)
        qT = sbt([128, 8, TT], BF16, WP + 4096)
        pbuf = sbt([128, 4, 2, TT], BF16, WP + 8192)
        rs = sbt([128, 4, TT], F32, WP + 12288)
        oT = sbt([128, 8, TT], BF16, WP + 16384)
        memT = sbt([128, 8, NMEM], F32, A0 + 32768)
        hm = sbt([128, 8, NMEM], BF16, WP + 12288)
        assert WP + 28672 - W0 <= WSIZE
        NS = 7
        sq8_3 = sbt([128, 8, TT], BF16, KT0)
        assert NS == 7
        actj = sbt([128, 4, TT], BF16, W0 + 15360)
        ysq8 = sbt([128, 8, TT], BF16, W0 + 17408)
        tmp4 = sbt([128, 4, TT], F32, W0 + 21504)
        wdjB = sbt([128, 3, D], BF16, W0 + 25600)
        wdjA = sbt([128, 4, D], BF16, W0 + 31744)
        sg = sbt([128, 2, TT], F32, W0 + 39936)
        assert W0 + 41984 - W0 <= WSIZE
        wd_slot = lambda s_: (wdjA[:, s_, :] if s_ < 4 else wdjB[:, s_ - 4, :])
        wd_slot_cols = lambda s_, c0, c1: (wdjA[:, s_, c0:c1] if s_ < 4 else wdjB[:, s_ - 4, c0:c1])

        ps = [stack.enter_context(nc.psum_tensor("ps%d" % i, [128, 512], F32)) for i in range(8)]

        def hb(bank, half, n=TT):
            return ps[bank][:, half * 256:half * 256 + n]

        def bk(bank):
            return "bk%d" % bank

        S.dma("sp", "small", lambda e: e.dma_start(out=sm[:], in_=small_d), writes=["small"])
        S.dma("sp", "ident", lambda e: e.dma_start(out=ident[:], in_=ident_d), writes=["ident"])
        S.dma("pool", "xh", lambda e: e.dma_start(out=xh_sb[:], in_=xT_d[:, :, 0:HALO]), writes=["xh"])
        S.dma("pool", "x0", lambda e: e.dma_start(out=x_sb[:, :, 0:TT], in_=xT_d[:, :, HALO:HALO + TT]),
              writes=["x%d_%d" % (c, 0) for c in range(8)])
        def load_group(semname, items):
            tok = None
            for (keys, fn) in items:
                keys = [keys] if isinstance(keys, str) else keys
                tok = S.dma("pool", semname, fn, writes=keys)
            for (keys, fn) in items:
                keys = [keys] if isinstance(keys, str) else keys
                for key in keys:
                    S.bufs[key]["w"] = tok

        def load_w(wsb, wdr, nm, c0=0, c1=None, keyp=None):
            c1 = wsb.shape[2] if c1 is None else c1
            keyp = nm if keyp is None else keyp
            src = wdr.rearrange("(kc p) n -> p kc n", p=128)
            items = []
            for half in range(2):
                k0, k1 = half * 4, half * 4 + 4
                fn = (lambda k0=k0, k1=k1: lambda e: e.dma_start(out=wsb[:, k0:k1, c0:c1], in_=src[:, k0:k1, c0:c1]))()
                items.append(([keyp + str(kc) for kc in range(k0, k1)], fn))
            load_group(nm, items)

        load_w(w_in, w_in_d, "w_in_cv", 512, 1536)
        load_w(w_in, w_in_d, "w_in_g", 1536, 2560)
        load_w(w_in, w_in_d, "w_in_b", 0, 512)
        S.dma("pool", "x1", lambda e: e.dma_start(out=x_sb[:, :, TT:2 * TT], in_=xT_d[:, :, HALO + TT:HALO + 2 * TT]),
              writes=["x%d_%d" % (c, 1) for c in range(8)])
        for t in range(2, NT):
            S.dma("sp", "x%d" % t,
                  (lambda t=t: lambda e: e.dma_start(out=x_sb[:, :, t * TT:(t + 1) * TT],
                                                     in_=xT_d[:, :, HALO + t * TT:HALO + (t + 1) * TT]))(),
                  reads=["w_in_b7"],
                  writes=["x%d_%d" % (c, t) for c in range(8)])

        S.op("dve", lambda e: e.memset(ones[:], 1.0), writes=["ones"])
        S.op("dve", lambda e: e.memset(epsc[:, 0:1], RMS_EPS), writes=["epsc"])
        S.op("dve", lambda e: e.memset(epsc[:, 1:2], LN_EPS), writes=["epsc"])
        S.op("dve", lambda e: e.tensor_scalar(out=sm[:, SM_CBW:SM_CBW + 124], in0=sm[:, SM_CBW:SM_CBW + 124],
                                              scalar1=0.5, scalar2=None, op0=ALU.mult),
             reads=["small"], writes=["small"])
        for part in range(4):
            S.op("dve", (lambda part=part: lambda e: e.tensor_tensor(
                out=diag[:, part * 34:(part + 1) * 34, :],
                in0=ident[:, :].unsqueeze(1).broadcast_to([128, 34, 128]),
                in1=sm[:, SM_CAW + part * 34:SM_CAW + (part + 1) * 34].unsqueeze(2).broadcast_to([128, 34, 128]),
                op=ALU.mult))(),
                reads=["small", "ident"], writes=["diag%d" % part])

        load_w(w_out, w_out_d, "w_out")
        load_group("wdcast", [("wdbf%d" % j, (lambda j=j: lambda e: e.dma_start(out=wd_bf[j * 128:(j + 1) * 128, :],
                                                                             in_=wd_d[j * 128:(j + 1) * 128, :]))())
                              for j in range(NJ)])
        wu_src = wu_d.rearrange("(kc p) n -> p kc n", p=128)

        def load_wu(j0, j1, semname):
            load_group(semname, [("wu%d" % j,
                                  (lambda j=j: lambda e: e.dma_start(out=wu[:, j, :, :],
                                                                     in_=wu_src[:, :, j * 128:(j + 1) * 128]))())
                                 for j in range(j0, j1)])

        NJ_ = NJ

        def load_wd(g):
            j = g % NJ_
            slot = g % NS
            S.dma("sp", "wdj%d" % slot,
                  (lambda j=j, slot=slot: lambda e: e.dma_start(out=wd_slot(slot),
                                                                in_=wd_bf[j * 128:(j + 1) * 128, :]))(),
                  reads=["wdbf%d" % j], writes=["wdj%d" % slot])

        def gcol(norm, c):
            return sm[:, SM_G + norm * 8 + c:SM_G + norm * 8 + c + 1]

        def PA(src_fn, src_keys, n, sq8):
            S.cur = "PA"
            for c in range(8):
                S.op("act", (lambda c=c: lambda e: e.activation(out=sq8[:, c, 0:n], in_=src_fn(c), func=AF.Square))(),
                     reads=[src_keys[c]], writes=["sq8_%d" % c])

        def PB(n, sq8):
            S.cur = "PB"
            for i, c in enumerate(reversed(range(8))):
                S.op("pe", (lambda c=c, i=i: lambda e: e.matmul(hb(7, 0, n), lhsT=ones[:], rhs=sq8[:, c, 0:n],
                                                               start=(i == 0), stop=(i == 7)))(),
                     reads=["ones", "sq8_%d" % c], writes=[bk(7)], signal=(i == 7))
            S.op("act", lambda e: e.activation(out=ms_pre[:, 0:n], in_=hb(7, 0, n), func=AF.Ln, scale=1.0 / D,
                                               bias=epsc[:, 0:1]),
                 reads=["epsc"], writes=["ms_pre", bk(7)])
            S.op("act", lambda e: e.activation(out=rstd_pre[:, 0:n], in_=ms_pre[:, 0:n], func=AF.Exp, scale=-0.5),
                 reads=["ms_pre"], writes=["rstd_pre"])

        def PC(src_fn, src_keys, n, norm, hp=0, halo=False):
            S.cur = "PC"
            for c in range(8):
                dst = h_halo[:, c, 0:n] if halo else hh[hp][:, c, 0:n]
                S.op("dve", (lambda c=c, dst=dst: lambda e: e.scalar_tensor_tensor(
                    out=dst, in0=src_fn(c), scalar=gcol(norm, c), in1=rstd_pre[:, 0:n],
                    op0=ALU.mult, op1=ALU.mult))(),
                    reads=[src_keys[c], "small", "rstd_pre"], writes=[("hh_%d" % c) if halo else ("h%d_%d" % (hp, c))])

        def proj(dst_bank, dst_half, n, w_sb, wkey, col0, rhs_fn, rhs_keys, nk=8, lhs_fn=None, wkeys=None, korder=None):
            if lhs_fn is None:
                lhs_fn = lambda kc: w_sb[:, kc, col0:col0 + 128]
            if wkeys is None:
                wkeys = [wkey + str(kc) for kc in range(nk)]
            korder = list(range(nk)) if korder is None else list(korder)
            for i, kc in enumerate(korder):
                S.op("pe", (lambda kc=kc, i=i: lambda e: e.matmul(
                    hb(dst_bank, dst_half, n), lhsT=lhs_fn(kc), rhs=rhs_fn(kc),
                    start=(i == 0), stop=(i == nk - 1)))(),
                    reads=[wkeys[kc], rhs_keys[kc]], writes=[bk(dst_bank)], signal=(i == nk - 1))

        ybank = lambda oc: (oc % 4, oc // 4)

        def ps_square(c, buf, nb, oc=None):
            b, hf = ybank(c if oc is None else oc)
            S.op("act", (lambda c=c, b=b, hf=hf: lambda e: e.activation(out=buf[:, c % nb, :], in_=hb(b, hf),
                                                                        func=AF.Square))(),
                 reads=[], writes=["ysq%d" % (c % nb), bk(b)])

        def ps_mm(c, buf, nb):
            S.op("pe", (lambda c=c: lambda e: e.matmul(hb(7, 1), lhsT=ones[:], rhs=buf[:, c % nb, :],
                                                      start=(c == 0), stop=(c == 7)))(),
                 reads=["ones", "ysq%d" % (c % nb)], writes=[bk(7)], signal=True)

        def ps_finish():
            S.op("act", lambda e: e.activation(out=ms_post[:], in_=hb(7, 1), func=AF.Ln, scale=1.0 / D, bias=epsc[:, 0:1]),
                 reads=["epsc"], writes=["ms_post", bk(7)])
            S.op("act", lambda e: e.activation(out=rstd_post[:], in_=ms_post[:], func=AF.Exp, scale=-0.5),
                 reads=["ms_post"], writes=["rstd_post"])

        def PN(t, norm, tbuf=None, nt=2, last=False):
            S.cur = "PN"
            tb = tmp if tbuf is None else tbuf
            for oc in range(8):
                b, hf = ybank(oc)
                S.op("dve", (lambda oc=oc, b=b, hf=hf: lambda e: e.scalar_tensor_tensor(
                    out=tb[:, oc % nt, :], in0=hb(b, hf), scalar=gcol(norm, oc), in1=rstd_post[:, :],
                    op0=ALU.mult, op1=ALU.mult))(),
                    reads=["small", "rstd_post"], writes=["tmp%d" % (oc % nt), bk(b)])
                xs = lambda oc=oc: x_sb[:, oc, t * TT:(t + 1) * TT]
                S.op("dve" if (last and oc % 2 == 1) else "pool", (lambda oc=oc, xs=xs: lambda e: e.tensor_tensor(
                    out=xs(), in0=xs(), in1=tb[:, oc % nt, :], op=ALU.add))(),
                    reads=["tmp%d" % (oc % nt)], writes=["x%d_%d" % (oc, t)])

        def outproj(w_sb, wkey, rhs_fn, rhs_keys, first_bank=0, korder=None):
            ocs = [((first_bank + i) % 4) + 4 * half for half in range(2) for i in range(4)]
            for idx, oc in enumerate(ocs):
                S.cur = "outproj"
                b, hf = ybank(oc)
                proj(b, hf, TT, w_sb, wkey, oc * 128, rhs_fn, rhs_keys, korder=korder)
                S.cur = "post_stats"
                ps_square(idx, ysq, 2, oc=oc)
                if idx >= 1:
                    ps_mm(idx - 1, ysq, 2)
            ps_mm(7, ysq, 2)
            ps_finish()

        xfn = lambda t: (lambda c: x_sb[:, c, t * TT:(t + 1) * TT])
        xkeys = lambda t: ["x%d_%d" % (c, t) for c in range(8)]
        h0k = ["h0_%d" % c for c in range(8)]

        sp1_keys = ["th0", "th1", "mu", "v1", "rstdB", "nmr", "t10", "t11", "t20", "t21",
                    "ca0", "ca1", "cat0", "cat1"] + \
                   ["zb%d_%d" % (p, c) for p in range(2) for c in range(4)] + ["zbh0", "zbh1", "cvh0", "cvh1"] + \
                   ["cv%d_%d" % (p, c) for p in range(2) for c in range(4)] + \
                   ["ya%d_%d" % (p, c) for p in range(2) for c in range(4)] + ["yb%d" % c for c in range(4)] + \
                   ["zc%d" % c for c in range(4)] + ["zq%d" % c for c in range(4)] + ["sq8_%d" % c for c in range(8)]
        w_in_keys = [p + str(k) for p in ("w_in_cv", "w_in_g", "w_in_b") for k in range(8)]
        regionA1 = w_in_keys + ["w_out%d" % k for k in range(8)] + ["diag%d" % p for p in range(4)]
        sp2_keys = ["qT%d" % c for c in range(8)] + ["p%d" % i for i in range(4)] + ["rs%d" % i for i in range(4)] + \
                   ["oT%d" % c for c in range(8)] + ["memT", "hm", "KT", "V"]
        COL_B, COL_C, COL_V, COL_GV, COL_GG = 0, 512, 1024, 1536, 2048

        def gu_pe(t, j):
            S.cur = "GU"
            hp = t % 2
            hk = ["h%d_%d" % (hp, c) for c in range(8)]
            hfn = lambda kc, hp=hp: hh[hp][:, kc, :]
            b = 4 + j % 3
            proj(b, 0, TT, wg, "wg", j * 128, hfn, hk)
            proj(b, 1, TT, None, None, 0, hfn, hk, lhs_fn=(lambda kc, j=j: wu[:, j, kc, :]),
                 wkeys=["wu%d" % j] * 8)

        def gu_ew(t, j):
            S.cur = "GU"
            b = 4 + j % 3
            S.op("act", (lambda j=j, b=b: lambda e: e.activation(out=sg[:, j % 2, :], in_=hb(b, 0),
                                                                 func=AF.Silu))(),
                 reads=[], writes=["sg%d" % (j % 2), bk(b)])
            S.op("dve", (lambda j=j, b=b: lambda e: e.tensor_tensor(out=actj[:, j % 4, :], in0=hb(b, 1),
                                                                    in1=sg[:, j % 2, :], op=ALU.mult))(),
                 reads=["sg%d" % (j % 2)], writes=["aj%d" % (j % 4), bk(b)])

        def norm_mem():
            S.cur = "KV"
            S.alias(["hm"], ["v1", "rstdB", "nmr", "t10", "t11", "mu"])
            for c in range(8):
                S.op("dve", (lambda c=c: lambda e: e.scalar_tensor_tensor(
                    out=hm[:, c, :], in0=memT[:, c, :], scalar=sm[:, SM_MEMG + c:SM_MEMG + c + 1],
                    in1=rstd_pre[:, 0:NMEM], op0=ALU.mult, op1=ALU.mult))(),
                    reads=["memT", "small", "rstd_pre"], writes=["hm"])

        def run_sp1():
            rr = [0]

            def fbank():
                rr[0] = (rr[0] + 1) % 4
                return rr[0]

            def F(n, par, dcol, halo_only, cs=(0, 1, 2, 3), parts=("cv", "glu", "a")):
                S.cur = "F"
                hfn = (lambda kc: h_halo[:, kc, 0:n]) if halo_only else (lambda kc: hh[0][:, kc, 0:n])
                h0k = ["hh_%d" % c for c in range(8)] if halo_only else ["h0_%d" % c for c in range(8)]
                zkey = (lambda c: "zbh%d" % par) if halo_only else (lambda c: "zb%d_%d" % (par, c))
                ckey = (lambda c: "cvh%d" % par) if halo_only else (lambda c: "cv%d_%d" % (par, c))
                for c in cs:
                    if "cv" in parts:
                        b = fbank()
                        proj(b, 0, n, w_in, "w_in_cv", COL_C + c * 128, hfn, h0k)
                        proj(b, 1, n, w_in, "w_in_cv", COL_V + c * 128, hfn, h0k)
                        S.op("act", (lambda c=c, b=b: lambda e: e.activation(out=ca[:, c % 2, 0:n], in_=hb(b, 0, n),
                                                                             func=AF.Copy))(),
                             reads=[], writes=["ca%d" % (c % 2), bk(b)])
                        S.op("dve", (lambda c=c, b=b: lambda e: e.tensor_tensor(
                            out=cvbuf[:, par, c, dcol:dcol + n], in0=hb(b, 1, n), in1=ca[:, c % 2, 0:n], op=ALU.mult))(),
                            reads=["ca%d" % (c % 2)], writes=[ckey(c), bk(b)])
                    if "a" in parts and not halo_only:
                        ckeys = ["cv%d_%d" % (par, c), "cvh%d" % par, "small"]
                        S.op("dve", (lambda c=c: lambda e: e.tensor_scalar(
                            out=cat[:, c % 2, :], in0=cvbuf[:, par, c, 30:30 + TT],
                            scalar1=sm[:, SM_CAW + c:SM_CAW + c + 1], scalar2=None, op0=ALU.mult))(),
                            reads=ckeys, writes=["cat%d" % (c % 2)])
                        for k in (1, 2):
                            S.op("dve", (lambda c=c, k=k: lambda e: e.scalar_tensor_tensor(
                                out=cat[:, c % 2, :], in0=cvbuf[:, par, c, 30 + k:30 + k + TT],
                                scalar=sm[:, SM_CAW + k * 4 + c:SM_CAW + k * 4 + c + 1], in1=cat[:, c % 2, :],
                                op0=ALU.mult, op1=ALU.add))(),
                                reads=ckeys + ["cat%d" % (c % 2)], writes=["cat%d" % (c % 2)])
                        b = fbank()
                        proj(b, 0, n, w_in, "w_in_b", COL_B + c * 128, hfn, h0k)
                        S.op("dve", (lambda c=c, b=b: lambda e: e.tensor_tensor(
                            out=ycat_a[:, par, c, :], in0=hb(b, 0), in1=cat[:, c % 2, :], op=ALU.mult))(),
                            reads=["cat%d" % (c % 2)], writes=["ya%d_%d" % (par, c), bk(b)])
                    if "glu" in parts:
                        b = fbank()
                        proj(b, 0, n, w_in, "w_in_g", COL_GG + c * 128, hfn, h0k)
                        proj(b, 1, n, w_in, "w_in_g", COL_GV + c * 128, hfn, h0k)
                        S.op("act", (lambda c=c, b=b: lambda e: e.activation(out=th[:, c % 2, 0:n], in_=hb(b, 0, n),
                                                                             func=AF.Tanh, scale=0.5))(),
                             reads=[], writes=["th%d" % (c % 2), bk(b)])
                        S.op("dve", (lambda c=c, b=b: lambda e: e.scalar_tensor_tensor(
                            out=zbuf[:, par, c, dcol:dcol + n], in0=th[:, c % 2, 0:n], scalar=1.0, in1=hb(b, 1, n),
                            op0=ALU.add, op1=ALU.mult))(),
                            reads=["th%d" % (c % 2)], writes=[zkey(c), bk(b)])

            def HC(par):
                S.op("pool", lambda e, par=par: e.tensor_copy(out=zbuf[:, 1 - par, :, 0:HALO],
                                                              in_=zbuf[:, par, :, TT:TT + HALO]),
                     reads=["zb%d_%d" % (par, c) for c in range(4)], writes=["zbh%d" % (1 - par)])
                S.op("pool", lambda e, par=par: e.tensor_copy(out=cvbuf[:, 1 - par, :, 0:HALO],
                                                              in_=cvbuf[:, par, :, TT:TT + HALO]),
                     reads=["cv%d_%d" % (par, c) for c in range(4)], writes=["cvh%d" % (1 - par)])

            def B1(par, chunks):
                S.cur = "B1"
                for c in chunks:
                    b, hf = 4 + c % 2, c // 2
                    for k in range(31):
                        di = 12 + k * 4 + c
                        S.op("pe", (lambda c=c, k=k, di=di, b=b, hf=hf, par=par: lambda e: e.matmul(
                            hb(b, hf), lhsT=diag[:, di, :], rhs=zbuf[:, par, c, 2 + k:2 + k + TT],
                            start=(k == 0), stop=(k == 30)))(),
                            reads=["diag%d" % (di // 34), "zb%d_%d" % (par, c), "zbh%d" % par], writes=[bk(b)],
                            signal=(k == 30))
                    S.op("act", (lambda c=c, b=b, hf=hf: lambda e: e.activation(
                        out=zc[:, c, :], in_=hb(b, hf), func=AF.Identity, bias=sm[:, SM_CBB + c:SM_CBB + c + 1]))(),
                        reads=["small"], writes=["zc%d" % c, bk(b)])
                    S.op("act", (lambda c=c, b=b, hf=hf: lambda e: e.activation(
                        out=zq[:, c, :], in_=hb(b, hf), func=AF.Square, bias=sm[:, SM_CBB + c:SM_CBB + c + 1]))(),
                        reads=["small"], writes=["zq%d" % c, bk(b)])

            def B2_stats():
                S.cur = "B2"
                for c in range(4):
                    S.op("pe", (lambda c=c: lambda e: e.matmul(hb(6, 0), lhsT=ones[:], rhs=zc[:, c, :],
                                                              start=(c == 0), stop=(c == 3)))(),
                         reads=["ones", "zc%d" % c], writes=[bk(6)], signal=(c == 3))
                for c in range(4):
                    S.op("pe", (lambda c=c: lambda e: e.matmul(hb(6, 1), lhsT=ones[:], rhs=zq[:, c, :],
                                                              start=(c == 0), stop=(c == 3)))(),
                         reads=["ones", "zq%d" % c], writes=[bk(6)], signal=(c == 3))
                S.op("dve", lambda e: e.tensor_scalar(out=mu[:], in0=hb(6, 0), scalar1=1.0 / 512, scalar2=None,
                                                      op0=ALU.mult),
                     writes=["mu", bk(6)])
                S.op("dve", lambda e: e.tensor_tensor(out=v1[:], in0=mu[:], in1=mu[:], op=ALU.mult),
                     reads=["mu"], writes=["v1"])
                S.op("dve", lambda e: e.scalar_tensor_tensor(out=v1[:], in0=hb(6, 1), scalar=1.0 / 512, in1=v1[:],
                                                             op0=ALU.mult, op1=ALU.subtract),
                     reads=["v1"], writes=["v1", bk(6)])
                S.op("act", lambda e: e.activation(out=v1[:], in_=v1[:], func=AF.Ln, bias=epsc[:, 1:2]),
                     reads=["v1", "epsc"], writes=["v1"])
                S.op("act", lambda e: e.activation(out=rstdB[:], in_=v1[:], func=AF.Exp, scale=-0.5),
                     reads=["v1"], writes=["rstdB"])
                S.op("dve", lambda e: e.scalar_tensor_tensor(out=nmr[:], in0=mu[:], scalar=-1.0, in1=rstdB[:],
                                                             op0=ALU.mult, op1=ALU.mult),
                     reads=["mu", "rstdB"], writes=["nmr"])

            def B2_chunk(c):
                S.cur = "B2"
                S.op("dve", (lambda c=c: lambda e: e.tensor_tensor(out=t1[:, c % 2, :], in0=zc[:, c, :], in1=rstdB[:],
                                                                   op=ALU.mult))(),
                     reads=["zc%d" % c, "rstdB"], writes=["t1%d" % (c % 2)])
                S.op("dve", (lambda c=c: lambda e: e.tensor_tensor(out=t2[:, c % 2, :], in0=t1[:, c % 2, :],
                                                                   in1=nmr[:], op=ALU.add))(),
                     reads=["t1%d" % (c % 2), "nmr"], writes=["t2%d" % (c % 2)])
                S.op("act", (lambda c=c: lambda e: e.activation(
                    out=ycat_b[:, c, :], in_=t2[:, c % 2, :], func=AF.Silu,
                    scale=sm[:, SM_LNG + c:SM_LNG + c + 1], bias=sm[:, SM_LNB + c:SM_LNB + c + 1]))(),
                    reads=["t2%d" % (c % 2), "small"], writes=["yb%d" % c])

            def O(t, par):
                yfn = lambda kc: ycat_a[:, par, kc, :] if kc < 4 else ycat_b[:, kc - 4, :]
                yk = ["ya%d_%d" % (par, c) for c in range(4)] + ["yb%d" % c for c in range(4)]
                outproj(w_out, "w_out", yfn, yk, first_bank=(rr[0] + 1) % 4, korder=(4, 5, 6, 7, 0, 1, 2, 3))
                PN(t, 1)

            hx = lambda c: xh_sb[:, c, :]
            PA(hx, ["xh"] * 8, HALO, sq8_1)
            PB(HALO, sq8_1)
            PA(xfn(0), xkeys(0), TT, sq8_1)
            PC(hx, ["xh"] * 8, HALO, 0, halo=True)
            PB(TT, sq8_1)
            PC(xfn(0), xkeys(0), TT, 0)
            F(HALO, 0, 0, True, parts=("cv",))
            F(TT, 0, HALO, False, parts=("cv",))
            F(HALO, 0, 0, True, parts=("glu",))
            F(TT, 0, HALO, False, parts=("glu",))
            F(TT, 0, HALO, False, parts=("a",))
            S.alias(["ysq0", "ysq1"], ["hh_%d" % c for c in range(8)])
            for t in range(NT):
                par = t % 2
                nxt = t + 1 < NT
                if nxt:
                    HC(par)
                    PA(xfn(t + 1), xkeys(t + 1), TT, sq8_1)
                elif NPHASES >= 2:
                    PA(xfn(0), xkeys(0), TT, sq8_1)
                B1(par, [0, 1] if t == 0 else [1, 2])
                if nxt:
                    PB(TT, sq8_1)
                    PC(xfn(t + 1), xkeys(t + 1), TT, 0)
                elif NPHASES >= 2:
                    PB(TT, sq8_1)
                    PC(xfn(0), xkeys(0), TT, 2)
                    PA(lambda c: memT[:, c, :], ["memT"] * 8, NMEM, sq8_1)
                B1(par, [2, 3] if t == 0 else [3])
                if not nxt and NPHASES >= 2:
                    PB(NMEM, sq8_1)
                    S.alias(["wq%d" % k for k in range(8)] + ["wo%d" % k for k in range(8)],
                            ["diag%d" % p for p in range(4)])
                    load_w(wq, wq_d, "wq")
                    load_w(wo, wo_d, "wo")
                B2_stats()
                if nxt:
                    F(TT, 1 - par, HALO, False, cs=(0,))
                    B2_chunk(0)
                    F(TT, 1 - par, HALO, False, cs=(1,))
                    B2_chunk(1)
                    F(TT, 1 - par, HALO, False, cs=(2,))
                    B2_chunk(2)
                    B2_chunk(3)
                    F(TT, 1 - par, HALO, False, cs=(3,))
                    B1(1 - par, [0])
                    if t + 2 == NT and NPHASES >= 2:
                        S.alias(["wk%d" % k for k in range(8)] + ["wv%d" % k for k in range(8)], w_in_keys)
                        load_w(wk, wk_d, "wk")
                        load_w(wv, wv_d, "wv")
                        S.alias(["memT"], w_in_keys)
                        S.dma("sp", "memT", lambda e: e.dma_start(out=memT[:], in_=memT_d), writes=["memT"])
                else:
                    for c in range(4):
                        B2_chunk(c)
                    if NPHASES >= 2:
                        norm_mem()
                O(t, par)

        def run_sp2():
            S.alias(sp2_keys + ["sq8_%d" % c for c in range(8)], sp1_keys)
            rr = [0]

            def sbank():
                rr[0] = (rr[0] + 1) % 3
                return 4 + rr[0]

            hfn = lambda kc: hh[0][:, kc, :]

            def Q(e2s=(0, 1, 2, 3)):
                S.cur = "Q"
                for e2 in e2s:
                    b = sbank()
                    for hf in range(2):
                        ec = e2 * 2 + hf
                        proj(b, hf, TT, wq, "wq", ec * 128, hfn, h0k)
                    S.op("act", (lambda e2=e2, b=b: lambda e: e.activation(
                        out=qT[:, 2 * e2:2 * e2 + 2, :], in_=ps[b][:, :].rearrange("p (a n) -> p a n", a=2),
                        func=AF.Copy))(),
                        reads=[], writes=["qT%d" % (2 * e2), "qT%d" % (2 * e2 + 1), bk(b)])

            Q()
            S.cur = "KV"
            for e2 in range(4):
                b = sbank()
                for hf in range(2):
                    ec = e2 * 2 + hf
                    proj(b, hf, NMEM, wk, "wk", ec * 128, lambda kc: hm[:, kc, :], ["hm"] * 8)
                S.op("act", (lambda e2=e2, b=b: lambda e: e.activation(
                    out=KT[:, 2 * e2:2 * e2 + 2, :], in_=ps[b][:, :].rearrange("p (a n) -> p a n", a=2), func=AF.Copy))(),
                    reads=[], writes=["KT", bk(b)])
            for mc in range(2):
                for eh in range(2):
                    b = sbank()
                    for kc in range(8):
                        S.op("pe", (lambda kc=kc, mc=mc, eh=eh, b=b: lambda e: e.matmul(
                            ps[b][:, :], lhsT=hm[:, kc, mc * 128:(mc + 1) * 128], rhs=wv[:, kc, eh * 512:(eh + 1) * 512],
                            start=(kc == 0), stop=(kc == 7)))(),
                            reads=["hm", "wv%d" % kc], writes=[bk(b)], signal=(kc == 7))
                    S.op("act", (lambda mc=mc, eh=eh, b=b: lambda e: e.activation(
                        out=Vt[:, mc, eh * 512:(eh + 1) * 512], in_=ps[b][:, :], func=AF.Copy))(),
                        reads=[], writes=["V", bk(b)])
            S.alias(["qT%d" % c for c in range(8)] + ["p%d" % i for i in range(4)] + ["rs%d" % i for i in range(4)],
                    ["memT", "hm"])
            S.alias(["wg%d" % k for k in range(8)],
                    ["wk%d" % k for k in range(8)] + ["wv%d" % k for k in range(8)] +
                    w_in_keys + ["w_out%d" % k for k in range(8)])
            load_w(wg, wg_d, "wg")
            S.alias(["wu%d" % j for j in range(0, 7)], ["w_out%d" % k for k in range(8)] + ["diag%d" % p for p in range(4)])
            load_wu(0, 7, "wu_a")

            def SC():
                S.cur = "SC"
                for hd in range(4):
                    b = sbank()
                    for mc in range(2):
                        for j in range(2):
                            ec = 2 * hd + j
                            S.op("pe", (lambda mc=mc, j=j, ec=ec, b=b: lambda e: e.matmul(
                                hb(b, mc), lhsT=KT[:, ec, mc * 128:(mc + 1) * 128], rhs=qT[:, ec, :],
                                start=(j == 0), stop=(j == 1)))(),
                                reads=["KT", "qT%d" % ec], writes=[bk(b)], signal=(j == 1))
                    S.op("act", (lambda hd=hd, b=b: lambda e: e.activation(
                        out=pbuf[:, hd, :, :], in_=ps[b][:, :].rearrange("p (a n) -> p a n", a=2), func=AF.Exp,
                        scale=1.0 / 16.0))(),
                        reads=[], writes=["p%d" % hd, bk(b)])

            def SM_PV(sb0=4):
                S.cur = "SM_PV"
                for hd in range(4):
                    for mc in range(2):
                        S.op("pe", (lambda mc=mc, hd=hd: lambda e: e.matmul(
                            hb(sb0 + hd // 2, hd % 2), lhsT=ones[:], rhs=pbuf[:, hd, mc, :], start=(mc == 0), stop=(mc == 1)))(),
                            reads=["ones", "p%d" % hd], writes=[bk(sb0 + hd // 2)], signal=(mc == 1))
                for hd in range(4):
                    for dc in range(2):
                        ec = 2 * hd + dc
                        for mc in range(2):
                            S.op("pe", (lambda mc=mc, ec=ec, dc=dc, hd=hd: lambda e: e.matmul(
                                hb(hd, dc), lhsT=Vt[:, mc, ec * 128:(ec + 1) * 128], rhs=pbuf[:, hd, mc, :],
                                start=(mc == 0), stop=(mc == 1)))(),
                                reads=["V", "p%d" % hd], writes=[bk(hd)], signal=(mc == 1))
                for h2 in range(2):
                    S.op("act", (lambda h2=h2: lambda e: e.activation(
                        out=rs[:, 2 * h2:2 * h2 + 2, :], in_=ps[sb0 + h2][:, :].rearrange("p (a n) -> p a n", a=2),
                        func=AF.Ln))(),
                        reads=[], writes=["rs%d" % (2 * h2), "rs%d" % (2 * h2 + 1), bk(sb0 + h2)])
                    S.op("act", (lambda h2=h2: lambda e: e.activation(
                        out=rs[:, 2 * h2:2 * h2 + 2, :], in_=rs[:, 2 * h2:2 * h2 + 2, :], func=AF.Exp, scale=-1.0))(),
                        reads=["rs%d" % (2 * h2), "rs%d" % (2 * h2 + 1)], writes=["rs%d" % (2 * h2), "rs%d" % (2 * h2 + 1)])
                for hd in range(4):
                    for dc in range(2):
                        ec = 2 * hd + dc
                        S.op("dve", (lambda ec=ec, dc=dc, hd=hd: lambda e: e.tensor_tensor(
                            out=oT[:, ec, :], in0=hb(hd, dc), in1=rs[:, hd, :], op=ALU.mult))(),
                            reads=["rs%d" % hd], writes=["oT%d" % ec, bk(hd)])

            def O(t):
                outproj(wo, "wo", lambda kc: oT[:, kc, :], ["oT%d" % c for c in range(8)])
                PN(t, 3)

            def PRE(t):
                PA(xfn(t), xkeys(t), TT, sq8_2)
                PB(TT, sq8_2)
                PC(xfn(t), xkeys(t), TT, 2)

            SC()
            if NT > 1:
                PRE(1)
            for t in range(NT):
                if t + 2 < NT:
                    PA(xfn(t + 2), xkeys(t + 2), TT, sq8_2)
                elif t + 2 == NT and NPHASES >= 3:
                    PA(xfn(0), xkeys(0), TT, sq8_2)
                    S.alias(["wdj%d" % i for i in range(4)], ["memT"] + sp1_keys)
                    for g in range(4):
                        load_wd(g)
                if t + 1 < NT:
                    Q()
                    if t + 2 == NT and NPHASES >= 3:
                        S.alias(["wu%d" % j for j in range(7, 15)], ["wq%d" % k for k in range(8)])
                        load_wu(7, 15, "wu_b")
                if t + 1 == NT and NPHASES >= 3:
                    gu_pe(0, 0)
                    gu_pe(0, 1)
                    SM_PV(sb0=6)
                    S.alias(["sg0", "sg1", "aj0", "aj1"], ["qT%d" % c for c in range(8)] + sp1_keys)
                    gu_ew(0, 0)
                    gu_ew(0, 1)
                else:
                    SM_PV()
                if t + 1 < NT:
                    SC()
                if t + 2 < NT:
                    PB(TT, sq8_2)
                    PC(xfn(t + 2), xkeys(t + 2), TT, 2)
                elif t + 2 == NT and NPHASES >= 3:
                    PB(TT, sq8_2)
                    PC(xfn(0), xkeys(0), TT, 4, hp=0)
                O(t)

        def run_sp3():
            sp3_keys = ["sg0", "sg1"] + ["aj%d" % i for i in range(4)] + ["wdj%d" % i for i in range(NS)] + \
                       ["h1_%d" % c for c in range(8)] + ["sq8_%d" % c for c in range(8)] + \
                       ["ysq%d" % c for c in range(8)] + ["tmp%d" % c for c in range(4)]
            S.alias(sp3_keys, sp2_keys + sp1_keys)
            NG = NT * NJ

            for g in range(4, NS):
                load_wd(g)
            S.alias(["wu%d" % j for j in range(15, NJ)], ["wo%d" % k for k in range(8)])
            load_wu(15, NJ, "wu_c")

            def down(t, j):
                S.cur = "down"
                g = t * NJ + j
                slot = g % NS
                for oc in range(8):
                    b, hf = ybank(oc)
                    S.op("pe", (lambda oc=oc, b=b, hf=hf, slot=slot, j=j: lambda e: e.matmul(
                        hb(b, hf), lhsT=wd_slot_cols(slot, oc * 128, (oc + 1) * 128), rhs=actj[:, j % 4, :],
                        start=(j == 0 and oc < 4), stop=(j == NJ - 1), skip_group_check=True))(),
                        reads=["wdj%d" % slot, "aj%d" % (j % 4)], writes=[bk(b)], signal=(oc == 7))
                if g + NS < NG:
                    load_wd(g + NS)

            def PRE_A(t):
                PA(xfn(t), xkeys(t), TT, sq8_3)

            def PRE_B(t):
                PB(TT, sq8_3)

            def PRE_C(t):
                PC(xfn(t), xkeys(t), TT, 4, hp=t % 2)

            out_toks = []
            LAG = 3
            if NT > 1:
                PRE_A(1)

            def finish_tile(t):
                S.cur = "post_stats"
                for i, c in enumerate(reversed(range(8))):
                    S.op("pe", (lambda c=c, i=i: lambda e: e.matmul(hb(7, 1), lhsT=ones[:], rhs=ysq8[:, c, :],
                                                                   start=(i == 0), stop=(i == 7)))(),
                         reads=["ones", "ysq%d" % c], writes=[bk(7)], signal=(i == 7))
                ps_finish()
                last = (t + 1 == NT)
                PN(t, 5, tbuf=tmp4, nt=4, last=last)
                if not last:
                    out_toks.append(S.dma("sp", "out",
                                          (lambda t=t: lambda e: e.dma_start(out=outT_d[:, :, t * TT:(t + 1) * TT],
                                                                             in_=x_sb[:, :, t * TT:(t + 1) * TT]))(),
                                          reads=["x%d_%d" % (c, t) for c in range(8)]))
                else:
                    for c2 in range(4):
                        out_toks.append(S.dma("sp", "out",
                                              (lambda t=t, c2=c2: lambda e: e.dma_start(
                                                  out=outT_d[:, 2 * c2:2 * c2 + 2, t * TT:(t + 1) * TT],
                                                  in_=x_sb[:, 2 * c2:2 * c2 + 2, t * TT:(t + 1) * TT]))(),
                                              reads=["x%d_%d" % (c, t) for c in (2 * c2, 2 * c2 + 1)]))

            for t in range(NT):
                hp = t % 2
                hk = ["h%d_%d" % (hp, c) for c in range(8)]
                hfn = lambda kc, hp=hp: hh[hp][:, kc, :]
                for j in range(NJ):
                    if j == 1 and t >= 1:
                        finish_tile(t - 1)
                    if not (t == 0 and j < 2):
                        gu_pe(t, j)
                        gu_ew(t, j)
                    if j == LAG and t + 1 < NT:
                        PRE_B(t + 1)
                        S.cur = "down"
                    if j >= LAG:
                        down(t, j - LAG)
                    if j == 10 and t + 2 < NT:
                        PRE_A(t + 2)
                    elif j == 12 and t + 1 < NT:
                        PRE_C(t + 1)
                for jj in range(NJ - LAG, NJ):
                    down(t, jj)
                S.cur = "post_stats"
                if t + 1 == NT:
                    S.op("act", lambda e: e.activation(out=ms_post[:, 0:1], in_=epsc[:, 0:1], func=AF.Ln),
                         reads=["epsc"], writes=["ms_post"])
                for c in range(8):
                    ps_square(c, ysq8, 8)
                if t + 1 == NT:
                    finish_tile(t)
            return out_toks

        out_toks = []
        if NPHASES >= 1:
            run_sp1()
        if NPHASES >= 2:
            run_sp2()
        if NPHASES >= 3:
            out_toks = run_sp3()
        else:
            for t in range(NT):
                out_toks.append(S.dma("sp", "out",
                                      (lambda t=t: lambda e: e.dma_start(out=outT_d[:, :, t * TT:(t + 1) * TT],
                                                                         in_=x_sb[:, :, t * TT:(t + 1) * TT]))(),
                                      reads=["x%d_%d" % (c, t) for c in range(8)]))
        S.wait_all("sp", out_toks)

        with nc.Block() as block:
            S.emit(block)
    return nc


_PROGRAM = None
_LAST_SCHED = None


def kernel(x, mem, mix_pre_g, w_mix_in, conv_a_w, conv_b_w, conv_b_b, ln_b_g, ln_b_b,
           w_mix_out, mix_post_g, xa_pre_g, mem_norm_g, w_q, w_k, w_v, w_o, xa_post_g,
           ffn_pre_g, w_gate, w_up, w_down, ffn_post_g):
    global _PROGRAM
    f = lambda a: np.ascontiguousarray(np.asarray(a, dtype=np.float32))
    x = f(x).reshape(SEQ, D)
    xp = np.concatenate([np.zeros((HALO, D), np.float32), x], axis=0)
    in_maps = []
    vec8 = lambda v: f(v).reshape(8, 128).T
    vec4 = lambda v: f(v).reshape(4, 128).T
    small = np.zeros((128, SM_N), np.float32)
    for i, g in enumerate([mix_pre_g, mix_post_g, xa_pre_g, xa_post_g, ffn_pre_g, ffn_post_g]):
        small[:, SM_G + i * 8:SM_G + (i + 1) * 8] = vec8(g)
    small[:, SM_MEMG:SM_MEMG + 8] = vec8(mem_norm_g)
    small[:, SM_CBB:SM_CBB + 4] = vec4(conv_b_b)
    small[:, SM_LNG:SM_LNG + 4] = vec4(ln_b_g)
    small[:, SM_LNB:SM_LNB + 4] = vec4(ln_b_b)
    caw = f(conv_a_w)
    cbw = f(conv_b_w)
    for k in range(3):
        small[:, SM_CAW + k * 4:SM_CAW + (k + 1) * 4] = vec4(caw[k])
    for k in range(31):
        small[:, SM_CBW + k * 4:SM_CBW + (k + 1) * 4] = vec4(cbw[k])
    ident = np.eye(128, dtype=np.float32)
    memT = np.ascontiguousarray(f(mem).reshape(NMEM, 8, 128).transpose(2, 1, 0))
    shared = {
        "small": small, "ident": ident, "memT": memT,
        "w_mix_in": f(w_mix_in), "w_mix_out": f(w_mix_out), "w_q": f(w_q), "w_k": f(w_k), "w_v": f(w_v),
        "w_o": f(w_o), "w_gate": f(w_gate), "w_up": f(w_up), "w_down": f(w_down),
    }
    for c in range(NCORES):
        seg = xp[c * TPC:c * TPC + HALO + TPC]
        xT = np.ascontiguousarray(seg.reshape(HALO + TPC, 8, 128).transpose(2, 1, 0))
        m = dict(shared)
        m["xT"] = xT
        in_maps.append(m)
    if _PROGRAM is None:
        _PROGRAM = build_program()
    res = run_bass_kernel_spmd(_PROGRAM, in_maps, core_ids=list(range(NCORES)))
    out = np.empty((SEQ, D), np.float32)
    for c in range(NCORES):
        oT = np.asarray(res.results[c]["outT"]).reshape(128, 8, TPC)
        out[c * TPC:(c + 1) * TPC] = oT.transpose(2, 1, 0).reshape(TPC, D)
    return out.reshape(1, SEQ, D)
```

```python
from contextlib import ExitStack
import numpy as np
import concourse.bass as bass
import concourse.mybir as mybir
from concourse.bass_utils import run_bass_kernel_spmd

F32 = mybir.dt.float32
BF16 = mybir.dt.bfloat16
ALU = mybir.AluOpType
AF = mybir.ActivationFunctionType

NCORES = 8
D = 1024
SEQ = 16384
TPC = SEQ // NCORES
TT = 256
NT = TPC // TT
HALO = 32
NMEM = 256
DFF = 2816
NJ = DFF // 128
DIN = 2560
RMS_EPS = 1e-6
LN_EPS = 1e-5

SM_G = 0
SM_MEMG = 48
SM_CBB = 56
SM_LNG = 60
SM_LNB = 64
SM_CAW = 68
SM_CBW = 80
SM_N = 204

ENGS = ["pe", "act", "dve", "pool", "sp"]
NPHASES = 3
SB_BASE = 17408


class Sched:
    def __init__(self, nc, stack):
        self.nc = nc
        self.stack = stack
        self.lists = {e: [] for e in ENGS}
        self.count = {e: 0 for e in ENGS}
        self.waited = {e: {} for e in ENGS}
        self.sems = {}
        self.dma_cum = {}
        self.bufs = {}
        self.known = {e: {} for e in ENGS}
        self.snap = {}
        self.seq = {}
        self.nrec = 0
        self.cur = ""
        for e in ENGS:
            self.sems[e] = stack.enter_context(nc.semaphore("s_" + e))

    def dma_sem(self, name):
        key = "dma_" + name
        if key not in self.sems:
            self.sems[key] = self.stack.enter_context(self.nc.semaphore(key))
            self.dma_cum[key] = 0
        return key

    def _deps(self, eng, reads, writes):
        toks = []
        for k in reads:
            st = self.bufs.get(k)
            if st and st["w"] is not None:
                toks.append(st["w"])
        for k in writes:
            st = self.bufs.get(k)
            if st:
                if st["w"] is not None:
                    toks.append(st["w"])
                toks.extend(st["r"].items())
        need = {}
        for (sk, v) in toks:
            if sk == "pe" and eng == "pe":
                continue
            if v > need.get(sk, 0):
                need[sk] = v
        waits = []
        kn = self.known[eng]
        for sk, v in sorted(need.items(), key=lambda it: -self.seq.get(it, 0)):
            if kn.get(sk, 0) >= v:
                continue
            waits.append((sk, v))
            kn[sk] = v
            sn = self.snap.get((sk, v))
            if sn:
                for k2, v2 in sn.items():
                    if v2 > kn.get(k2, 0):
                        kn[k2] = v2
        return waits

    def _update(self, tok, reads, writes):
        for k in reads:
            st = self.bufs.setdefault(k, {"w": None, "r": {}})
            if tok[1] > st["r"].get(tok[0], 0):
                st["r"][tok[0]] = tok[1]
        for k in writes:
            self.bufs[k] = {"w": tok, "r": {}}

    def alias(self, newkeys, oldkeys):
        merged = {}
        for k in oldkeys:
            st = self.bufs.get(k)
            if not st:
                continue
            items = list(st["r"].items())
            if st["w"] is not None:
                items.append(st["w"])
            for sk, v in items:
                if v > merged.get(sk, 0):
                    merged[sk] = v
        for k in newkeys:
            st = self.bufs.setdefault(k, {"w": None, "r": {}})
            for sk, v in merged.items():
                if v > st["r"].get(sk, 0):
                    st["r"][sk] = v

    def op(self, eng, fn, reads=(), writes=(), signal=True):
        waits = self._deps(eng, reads, writes)
        if signal:
            self.count[eng] += 1
            tok = (eng, self.count[eng])
            inc = (eng, 1)
            sn = dict(self.known[eng])
            sn.pop(eng, None)
            self.snap[tok] = sn
            self.nrec += 1
            self.seq[tok] = self.nrec
        else:
            tok = (eng, self.count[eng] + 1)
            inc = None
        self.lists[eng].append((waits, fn, inc, self.cur))
        self._update(tok, reads, writes)
        return tok

    def dma(self, eng, semname, fn, reads=(), writes=()):
        key = self.dma_sem(semname)
        waits = self._deps(eng, reads, writes)
        self.dma_cum[key] += 16
        tok = (key, self.dma_cum[key])
        sn = dict(self.known[eng])
        sn.pop(eng, None)
        self.snap[tok] = sn
        self.nrec += 1
        self.seq[tok] = self.nrec
        self.lists[eng].append((waits, fn, (key, 16), self.cur))
        self._update(tok, reads, writes)
        return tok

    def wait_all(self, eng, toks):
        need = {}
        for (sk, v) in toks:
            if v > need.get(sk, 0):
                need[sk] = v
        self.lists[eng].append((list(need.items()), None, None, self.cur))

    def emit(self, block):
        sems = self.sems
        lists = self.lists

        def run(engname):
            def body(e):
                for (waits, fn, inc, _lbl) in lists[engname]:
                    for (sk, v) in waits:
                        e.wait_ge(sems[sk], v)
                    if fn is not None:
                        ins = fn(e)
                        if inc is not None:
                            ins.then_inc(sems[inc[0]], inc[1])
            return body

        block.tensor(run("pe"))
        block.scalar(run("act"))
        block.vector(run("dve"))
        block.gpsimd(run("pool"))
        block.sync(run("sp"))


def build_program():
    nc = bass.Bass("TRN2", target_bir_lowering=False)
    dr = lambda name, shape, kind="ExternalInput": nc.dram_tensor(name, shape, F32, kind=kind).ap()
    xT_d = dr("xT", [128, 8, HALO + TPC])
    small_d = dr("small", [128, SM_N])
    ident_d = dr("ident", [128, 128])
    memT_d = dr("memT", [128, 8, NMEM])
    w_in_d = dr("w_mix_in", [D, DIN])
    w_out_d = dr("w_mix_out", [D, D])
    wq_d = dr("w_q", [D, D])
    wk_d = dr("w_k", [D, D])
    wv_d = dr("w_v", [D, D])
    wo_d = dr("w_o", [D, D])
    wg_d = dr("w_gate", [D, DFF])
    wu_d = dr("w_up", [D, DFF])
    wd_d = dr("w_down", [DFF, D])
    outT_d = dr("outT", [128, 8, TPC], kind="ExternalOutput")
    wd_bf = nc.dram_tensor("wd_bf16", [DFF, D], BF16).ap()

    with ExitStack() as stack:
        S = Sched(nc, stack)
        global _LAST_SCHED
        _LAST_SCHED = S
        _n = [0]

        def sbt(shape, dt, off):
            _n[0] += 1
            nbytes = int(np.prod(shape[1:])) * (4 if dt == F32 else 2)
            assert SB_BASE + off + nbytes <= 229344, (shape, off)
            return nc.alloc_sbuf_tensor_at("t%d" % _n[0], list(shape), dt, offset=SB_BASE + off)

        X0 = 0
        XH0 = 65536
        SM0 = 66560
        ID0 = 67392
        ON0 = 67904
        NH0 = 68160
        KT0 = 69184
        V0 = 73280
        A0 = 77376
        W0 = 169536
        WSIZE = 229344 - SB_BASE - W0
        x_sb = sbt([128, 8, TPC], F32, X0)
        xh_sb = sbt([128, 8, HALO], F32, XH0)
        sm = sbt([128, SM_N], F32, SM0)
        ident = sbt([128, 128], F32, ID0)
        ones = sbt([128, 128], BF16, ON0)
        epsc = sbt([128, 2], F32, NH0)
        KT = sbt([128, 8, NMEM], BF16, KT0)
        Vt = sbt([128, 2, D], BF16, V0)
        w_in = sbt([128, 8, DIN], BF16, A0)
        w_out = sbt([128, 8, D], BF16, A0 + 40960)
        diag = sbt([128, 136, 128], BF16, A0 + 57344)
        wk = sbt([128, 8, D], BF16, A0)
        wv = sbt([128, 8, D], BF16, A0 + 16384)
        wq = sbt([128, 8, D], BF16, A0 + 59392)
        wo = sbt([128, 8, D], BF16, A0 + 75776)
        wg = sbt([128, 8, DFF], BF16, A0)
        wu = sbt([128, NJ, 8, 128], BF16, A0 + 45056)
        ms_pre = sbt([128, TT], F32, W0)
        rstd_pre = sbt([128, TT], F32, W0 + 1024)
        ms_post = sbt([128, TT], F32, W0 + 2048)
        rstd_post = sbt([128, TT], F32, W0 + 3072)
        hh = [sbt([128, 8, TT], BF16, W0 + 4096), sbt([128, 8, TT], BF16, W0 + 11264)]
        ysq = sbt([128, 2, TT], BF16, W0 + 8192)
        h_halo = sbt([128, 8, HALO], BF16, W0 + 8192)
        tmp = sbt([128, 2, TT], F32, W0 + 9216)
        WP = W0 + 11264
        WP3 = W0 + 15360
        sq8_1 = sbt([128, 8, TT], BF16, KT0)
        ycat_a = sbt([128, 2, 4, TT], BF16, V0)
        th = sbt([128, 2, TT], F32, WP)
        zbuf = sbt([128, 2, 4, HALO + TT], BF16, WP + 2048)
        zc = sbt([128, 4, TT], BF16, WP + 6656)
        zq = sbt([128, 4, TT], BF16, WP + 8704)
        mu = sbt([128, TT], F32, WP + 10752)
        v1 = sbt([128, TT], F32, WP + 11776)
        rstdB = sbt([128, TT], F32, WP + 12800)
        nmr = sbt([128, TT], F32, WP + 13824)
        t1 = sbt([128, 2, TT], F32, WP + 14848)
        t2 = sbt([128, 2, TT], F32, WP + 16896)
        ca = sbt([128, 2, TT], F32, WP + 18944)
        cat = sbt([128, 2, TT], F32, WP + 20992)
        cvbuf = sbt([128, 2, 4, HALO + TT], BF16, WP + 23040)
        ycat_b = sbt([128, 4, TT], BF16, WP + 27648)
        assert WP + 29696 - W0 <= WSIZE
        sq8_2 = sbt([128, 8, TT], BF16, WP)
        qT = sbt([128, 8, TT], BF16, WP + 4096)
        pbuf = sbt([128, 4, 2, TT], BF16, WP + 8192)
        rs = sbt([128, 4, TT], F32, WP + 12288)
        oT = sbt([128, 8, TT], BF16, WP + 16384)
        memT = sbt([128, 8, NMEM], F32, A0 + 32768)
        hm = sbt([128, 8, NMEM], BF16, WP + 12288)
        assert WP + 28672 - W0 <= WSIZE
        NS = 7
        sq8_3 = sbt([128, 8, TT], BF16, KT0)
        assert NS == 7
        actj = sbt([128, 4, TT], BF16, W0 + 15360)
        ysq8 = sbt([128, 8, TT], BF16, W0 + 17408)
        tmp4 = sbt([128, 4, TT], F32, W0 + 21504)
        wdjB = sbt([128, 3, D], BF16, W0 + 25600)
        wdjA = sbt([128, 4, D], BF16, W0 + 31744)
        sg = sbt([128, 2, TT], F32, W0 + 39936)
        assert W0 + 41984 - W0 <= WSIZE
        wd_slot = lambda s_: (wdjA[:, s_, :] if s_ < 4 else wdjB[:, s_ - 4, :])
        wd_slot_cols = lambda s_, c0, c1: (wdjA[:, s_, c0:c1] if s_ < 4 else wdjB[:, s_ - 4, c0:c1])

        ps = [stack.enter_context(nc.psum_tensor("ps%d" % i, [128, 512], F32)) for i in range(8)]

        def hb(bank, half, n=TT):
            return ps[bank][:, half * 256:half * 256 + n]

        def bk(bank):
            return "bk%d" % bank

        S.dma("sp", "small", lambda e: e.dma_start(out=sm[:], in_=small_d), writes=["small"])
        S.dma("sp", "ident", lambda e: e.dma_start(out=ident[:], in_=ident_d), writes=["ident"])
        S.dma("pool", "xh", lambda e: e.dma_start(out=xh_sb[:], in_=xT_d[:, :, 0:HALO]), writes=["xh"])
        S.dma("pool", "x0", lambda e: e.dma_start(out=x_sb[:, :, 0:TT], in_=xT_d[:, :, HALO:HALO + TT]),
              writes=["x%d_%d" % (c, 0) for c in range(8)])
        def load_group(semname, items):
            tok = None
            for (keys, fn) in items:
                keys = [keys] if isinstance(keys, str) else keys
                tok = S.dma("pool", semname, fn, writes=keys)
            for (keys, fn) in items:
                keys = [keys] if isinstance(keys, str) else keys
                for key in keys:
                    S.bufs[key]["w"] = tok

        def load_w(wsb, wdr, nm, c0=0, c1=None, keyp=None):
            c1 = wsb.shape[2] if c1 is None else c1
            keyp = nm if keyp is None else keyp
            src = wdr.rearrange("(kc p) n -> p kc n", p=128)
            items = []
            for half in range(2):
                k0, k1 = half * 4, half * 4 + 4
                fn = (lambda k0=k0, k1=k1: lambda e: e.dma_start(out=wsb[:, k0:k1, c0:c1], in_=src[:, k0:k1, c0:c1]))()
                items.append(([keyp + str(kc) for kc in range(k0, k1)], fn))
            load_group(nm, items)

        load_w(w_in, w_in_d, "w_in_cv", 512, 1536)
        load_w(w_in, w_in_d, "w_in_g", 1536, 2560)
        load_w(w_in, w_in_d, "w_in_b", 0, 512)
        S.dma("pool", "x1", lambda e: e.dma_start(out=x_sb[:, :, TT:2 * TT], in_=xT_d[:, :, HALO + TT:HALO + 2 * TT]),
              writes=["x%d_%d" % (c, 1) for c in range(8)])
        for t in range(2, NT):
            S.dma("sp", "x%d" % t,
                  (lambda t=t: lambda e: e.dma_start(out=x_sb[:, :, t * TT:(t + 1) * TT],
                                                     in_=xT_d[:, :, HALO + t * TT:HALO + (t + 1) * TT]))(),
                  reads=["w_in_b7"],
                  writes=["x%d_%d" % (c, t) for c in range(8)])

        S.op("dve", lambda e: e.memset(ones[:], 1.0), writes=["ones"])
        S.op("dve", lambda e: e.memset(epsc[:, 0:1], RMS_EPS), writes=["epsc"])
        S.op("dve", lambda e: e.memset(epsc[:, 1:2], LN_EPS), writes=["epsc"])
        S.op("dve", lambda e: e.tensor_scalar(out=sm[:, SM_CBW:SM_CBW + 124], in0=sm[:, SM_CBW:SM_CBW + 124],
                                              scalar1=0.5, scalar2=None, op0=ALU.mult),
             reads=["small"], writes=["small"])
        for part in range(4):
            S.op("dve", (lambda part=part: lambda e: e.tensor_tensor(
                out=diag[:, part * 34:(part + 1) * 34, :],
                in0=ident[:, :].unsqueeze(1).broadcast_to([128, 34, 128]),
                in1=sm[:, SM_CAW + part * 34:SM_CAW + (part + 1) * 34].unsqueeze(2).broadcast_to([128, 34, 128]),
                op=ALU.mult))(),
                reads=["small", "ident"], writes=["diag%d" % part])

        load_w(w_out, w_out_d, "w_out")
        load_group("wdcast", [("wdbf%d" % j, (lambda j=j: lambda e: e.dma_start(out=wd_bf[j * 128:(j + 1) * 128, :],
                                                                             in_=wd_d[j * 128:(j + 1) * 128, :]))())
                              for j in range(NJ)])
        wu_src = wu_d.rearrange("(kc p) n -> p kc n", p=128)

        def load_wu(j0, j1, semname):
            load_group(semname, [("wu%d" % j,
                                  (lambda j=j: lambda e: e.dma_start(out=wu[:, j, :, :],
                                                                     in_=wu_src[:, :, j * 128:(j + 1) * 128]))())
                                 for j in range(j0, j1)])

        NJ_ = NJ

        def load_wd(g):
            j = g % NJ_
            slot = g % NS
            S.dma("sp", "wdj%d" % slot,
                  (lambda j=j, slot=slot: lambda e: e.dma_start(out=wd_slot(slot),
                                                                in_=wd_bf[j * 128:(j + 1) * 128, :]))(),
                  reads=["wdbf%d" % j], writes=["wdj%d" % slot])

        def gcol(norm, c):
            return sm[:, SM_G + norm * 8 + c:SM_G + norm * 8 + c + 1]

        def PA(src_fn, src_keys, n, sq8):
            S.cur = "PA"
            for c in range(8):
                S.op("act", (lambda c=c: lambda e: e.activation(out=sq8[:, c, 0:n], in_=src_fn(c), func=AF.Square))(),
                     reads=[src_keys[c]], writes=["sq8_%d" % c])

        def PB(n, sq8):
            S.cur = "PB"
            for i, c in enumerate(reversed(range(8))):
                S.op("pe", (lambda c=c, i=i: lambda e: e.matmul(hb(7, 0, n), lhsT=ones[:], rhs=sq8[:, c, 0:n],
                                                               start=(i == 0), stop=(i == 7)))(),
                     reads=["ones", "sq8_%d" % c], writes=[bk(7)], signal=(i == 7))
            S.op("act", lambda e: e.activation(out=ms_pre[:, 0:n], in_=hb(7, 0, n), func=AF.Ln, scale=1.0 / D,
                                               bias=epsc[:, 0:1]),
                 reads=["epsc"], writes=["ms_pre", bk(7)])
            S.op("act", lambda e: e.activation(out=rstd_pre[:, 0:n], in_=ms_pre[:, 0:n], func=AF.Exp, scale=-0.5),
                 reads=["ms_pre"], writes=["rstd_pre"])

        def PC(src_fn, src_keys, n, norm, hp=0, halo=False):
            S.cur = "PC"
            for c in range(8):
                dst = h_halo[:, c, 0:n] if halo else hh[hp][:, c, 0:n]
                S.op("dve", (lambda c=c, dst=dst: lambda e: e.scalar_tensor_tensor(
                    out=dst, in0=src_fn(c), scalar=gcol(norm, c), in1=rstd_pre[:, 0:n],
                    op0=ALU.mult, op1=ALU.mult))(),
                    reads=[src_keys[c], "small", "rstd_pre"], writes=[("hh_%d" % c) if halo else ("h%d_%d" % (hp, c))])

        def proj(dst_bank, dst_half, n, w_sb, wkey, col0, rhs_fn, rhs_keys, nk=8, lhs_fn=None, wkeys=None, korder=None):
            if lhs_fn is None:
                lhs_fn = lambda kc: w_sb[:, kc, col0:col0 + 128]
            if wkeys is None:
                wkeys = [wkey + str(kc) for kc in range(nk)]
            korder = list(range(nk)) if korder is None else list(korder)
            for i, kc in enumerate(korder):
                S.op("pe", (lambda kc=kc, i=i: lambda e: e.matmul(
                    hb(dst_bank, dst_half, n), lhsT=lhs_fn(kc), rhs=rhs_fn(kc),
                    start=(i == 0), stop=(i == nk - 1)))(),
                    reads=[wkeys[kc], rhs_keys[kc]], writes=[bk(dst_bank)], signal=(i == nk - 1))

        ybank = lambda oc: (oc % 4, oc // 4)

        def ps_square(c, buf, nb, oc=None):
            b, hf = ybank(c if oc is None else oc)
            S.op("act", (lambda c=c, b=b, hf=hf: lambda e: e.activation(out=buf[:, c % nb, :], in_=hb(b, hf),
                                                                        func=AF.Square))(),
                 reads=[], writes=["ysq%d" % (c % nb), bk(b)])

        def ps_mm(c, buf, nb):
            S.op("pe", (lambda c=c: lambda e: e.matmul(hb(7, 1), lhsT=ones[:], rhs=buf[:, c % nb, :],
                                                      start=(c == 0), stop=(c == 7)))(),
                 reads=["ones", "ysq%d" % (c % nb)], writes=[bk(7)], signal=True)

        def ps_finish():
            S.op("act", lambda e: e.activation(out=ms_post[:], in_=hb(7, 1), func=AF.Ln, scale=1.0 / D, bias=epsc[:, 0:1]),
                 reads=["epsc"], writes=["ms_post", bk(7)])
            S.op("act", lambda e: e.activation(out=rstd_post[:], in_=ms_post[:], func=AF.Exp, scale=-0.5),
                 reads=["ms_post"], writes=["rstd_post"])

        def PN(t, norm, tbuf=None, nt=2, last=False):
            S.cur = "PN"
            tb = tmp if tbuf is None else tbuf
            for oc in range(8):
                b, hf = ybank(oc)
                S.op("dve", (lambda oc=oc, b=b, hf=hf: lambda e: e.scalar_tensor_tensor(
                    out=tb[:, oc % nt, :], in0=hb(b, hf), scalar=gcol(norm, oc), in1=rstd_post[:, :],
                    op0=ALU.mult, op1=ALU.mult))(),
                    reads=["small", "rstd_post"], writes=["tmp%d" % (oc % nt), bk(b)])
                xs = lambda oc=oc: x_sb[:, oc, t * TT:(t + 1) * TT]
                S.op("dve" if (last and oc % 2 == 1) else "pool", (lambda oc=oc, xs=xs: lambda e: e.tensor_tensor(
                    out=xs(), in0=xs(), in1=tb[:, oc % nt, :], op=ALU.add))(),
                    reads=["tmp%d" % (oc % nt)], writes=["x%d_%d" % (oc, t)])

        def outproj(w_sb, wkey, rhs_fn, rhs_keys, first_bank=0, korder=None):
            ocs = [((first_bank + i) % 4) + 4 * half for half in range(2) for i in range(4)]
            for idx, oc in enumerate(ocs):
                S.cur = "outproj"
                b, hf = ybank(oc)
                proj(b, hf, TT, w_sb, wkey, oc * 128, rhs_fn, rhs_keys, korder=korder)
                S.cur = "post_stats"
                ps_square(idx, ysq, 2, oc=oc)
                if idx >= 1:
                    ps_mm(idx - 1, ysq, 2)
            ps_mm(7, ysq, 2)
            ps_finish()

        xfn = lambda t: (lambda c: x_sb[:, c, t * TT:(t + 1) * TT])
        xkeys = lambda t: ["x%d_%d" % (c, t) for c in range(8)]
        h0k = ["h0_%d" % c for c in range(8)]

        sp1_keys = ["th0", "th1", "mu", "v1", "rstdB", "nmr", "t10", "t11", "t20", "t21",
                    "ca0", "ca1", "cat0", "cat1"] + \
                   ["zb%d_%d" % (p, c) for p in range(2) for c in range(4)] + ["zbh0", "zbh1", "cvh0", "cvh1"] + \
                   ["cv%d_%d" % (p, c) for p in range(2) for c in range(4)] + \
                   ["ya%d_%d" % (p, c) for p in range(2) for c in range(4)] + ["yb%d" % c for c in range(4)] + \
                   ["zc%d" % c for c in range(4)] + ["zq%d" % c for c in range(4)] + ["sq8_%d" % c for c in range(8)]
        w_in_keys = [p + str(k) for p in ("w_in_cv", "w_in_g", "w_in_b") for k in range(8)]
        regionA1 = w_in_keys + ["w_out%d" % k for k in range(8)] + ["diag%d" % p for p in range(4)]
        sp2_keys = ["qT%d" % c for c in range(8)] + ["p%d" % i for i in range(4)] + ["rs%d" % i for i in range(4)] + \
                   ["oT%d" % c for c in range(8)] + ["memT", "hm", "KT", "V"]
        COL_B, COL_C, COL_V, COL_GV, COL_GG = 0, 512, 1024, 1536, 2048

        def gu_pe(t, j):
            S.cur = "GU"
            hp = t % 2
            hk = ["h%d_%d" % (hp, c) for c in range(8)]
            hfn = lambda kc, hp=hp: hh[hp][:, kc, :]
            b = 4 + j % 3
            proj(b, 0, TT, wg, "wg", j * 128, hfn, hk)
            proj(b, 1, TT, None, None, 0, hfn, hk, lhs_fn=(lambda kc, j=j: wu[:, j, kc, :]),
                 wkeys=["wu%d" % j] * 8)

        def gu_ew(t, j):
            S.cur = "GU"
            b = 4 + j % 3
            S.op("act", (lambda j=j, b=b: lambda e: e.activation(out=sg[:, j % 2, :], in_=hb(b, 0),
                                                                 func=AF.Silu))(),
                 reads=[], writes=["sg%d" % (j % 2), bk(b)])
            S.op("dve", (lambda j=j, b=b: lambda e: e.tensor_tensor(out=actj[:, j % 4, :], in0=hb(b, 1),
                                                                    in1=sg[:, j % 2, :], op=ALU.mult))(),
                 reads=["sg%d" % (j % 2)], writes=["aj%d" % (j % 4), bk(b)])

        def norm_mem():
            S.cur = "KV"
            S.alias(["hm"], ["v1", "rstdB", "nmr", "t10", "t11", "mu"])
            for c in range(8):
                S.op("dve", (lambda c=c: lambda e: e.scalar_tensor_tensor(
                    out=hm[:, c, :], in0=memT[:, c, :], scalar=sm[:, SM_MEMG + c:SM_MEMG + c + 1],
                    in1=rstd_pre[:, 0:NMEM], op0=ALU.mult, op1=ALU.mult))(),
                    reads=["memT", "small", "rstd_pre"], writes=["hm"])

        def run_sp1():
            rr = [0]

            def fbank():
                rr[0] = (rr[0] + 1) % 4
                return rr[0]

            def F(n, par, dcol, halo_only, cs=(0, 1, 2, 3), parts=("cv", "glu", "a")):
                S.cur = "F"
                hfn = (lambda kc: h_halo[:, kc, 0:n]) if halo_only else (lambda kc: hh[0][:, kc, 0:n])
                h0k = ["hh_%d" % c for c in range(8)] if halo_only else ["h0_%d" % c for c in range(8)]
                zkey = (lambda c: "zbh%d" % par) if halo_only else (lambda c: "zb%d_%d" % (par, c))
                ckey = (lambda c: "cvh%d" % par) if halo_only else (lambda c: "cv%d_%d" % (par, c))
                for c in cs:
                    if "cv" in parts:
                        b = fbank()
                        proj(b, 0, n, w_in, "w_in_cv", COL_C + c * 128, hfn, h0k)
                        proj(b, 1, n, w_in, "w_in_cv", COL_V + c * 128, hfn, h0k)
                        S.op("act", (lambda c=c, b=b: lambda e: e.activation(out=ca[:, c % 2, 0:n], in_=hb(b, 0, n),
                                                                             func=AF.Copy))(),
                             reads=[], writes=["ca%d" % (c % 2), bk(b)])
                        S.op("dve", (lambda c=c, b=b: lambda e: e.tensor_tensor(
                            out=cvbuf[:, par, c, dcol:dcol + n], in0=hb(b, 1, n), in1=ca[:, c % 2, 0:n], op=ALU.mult))(),
                            reads=["ca%d" % (c % 2)], writes=[ckey(c), bk(b)])
                    if "a" in parts and not halo_only:
                        ckeys = ["cv%d_%d" % (par, c), "cvh%d" % par, "small"]
                        S.op("dve", (lambda c=c: lambda e: e.tensor_scalar(
                            out=cat[:, c % 2, :], in0=cvbuf[:, par, c, 30:30 + TT],
                            scalar1=sm[:, SM_CAW + c:SM_CAW + c + 1], scalar2=None, op0=ALU.mult))(),
                            reads=ckeys, writes=["cat%d" % (c % 2)])
                        for k in (1, 2):
                            S.op("dve", (lambda c=c, k=k: lambda e: e.scalar_tensor_tensor(
                                out=cat[:, c % 2, :], in0=cvbuf[:, par, c, 30 + k:30 + k + TT],
                                scalar=sm[:, SM_CAW + k * 4 + c:SM_CAW + k * 4 + c + 1], in1=cat[:, c % 2, :],
                                op0=ALU.mult, op1=ALU.add))(),
                                reads=ckeys + ["cat%d" % (c % 2)], writes=["cat%d" % (c % 2)])
                        b = fbank()
                        proj(b, 0, n, w_in, "w_in_b", COL_B + c * 128, hfn, h0k)
                        S.op("dve", (lambda c=c, b=b: lambda e: e.tensor_tensor(
                            out=ycat_a[:, par, c, :], in0=hb(b, 0), in1=cat[:, c % 2, :], op=ALU.mult))(),
                            reads=["cat%d" % (c % 2)], writes=["ya%d_%d" % (par, c), bk(b)])
                    if "glu" in parts:
                        b = fbank()
                        proj(b, 0, n, w_in, "w_in_g", COL_GG + c * 128, hfn, h0k)
                        proj(b, 1, n, w_in, "w_in_g", COL_GV + c * 128, hfn, h0k)
                        S.op("act", (lambda c=c, b=b: lambda e: e.activation(out=th[:, c % 2, 0:n], in_=hb(b, 0, n),
                                                                             func=AF.Tanh, scale=0.5))(),
                             reads=[], writes=["th%d" % (c % 2), bk(b)])
                        S.op("dve", (lambda c=c, b=b: lambda e: e.scalar_tensor_tensor(
                            out=zbuf[:, par, c, dcol:dcol + n], in0=th[:, c % 2, 0:n], scalar=1.0, in1=hb(b, 1, n),
                            op0=ALU.add, op1=ALU.mult))(),
                            reads=["th%d" % (c % 2)], writes=[zkey(c), bk(b)])

            def HC(par):
                S.op("pool", lambda e, par=par: e.tensor_copy(out=zbuf[:, 1 - par, :, 0:HALO],
                                                              in_=zbuf[:, par, :, TT:TT + HALO]),
                     reads=["zb%d_%d" % (par, c) for c in range(4)], writes=["zbh%d" % (1 - par)])
                S.op("pool", lambda e, par=par: e.tensor_copy(out=cvbuf[:, 1 - par, :, 0:HALO],
                                                              in_=cvbuf[:, par, :, TT:TT + HALO]),
                     reads=["cv%d_%d" % (par, c) for c in range(4)], writes=["cvh%d" % (1 - par)])

            def B1(par, chunks):
                S.cur = "B1"
                for c in chunks:
                    b, hf = 4 + c % 2, c // 2
                    for k in range(31):
                        di = 12 + k * 4 + c
                        S.op("pe", (lambda c=c, k=k, di=di, b=b, hf=hf, par=par: lambda e: e.matmul(
                            hb(b, hf), lhsT=diag[:, di, :], rhs=zbuf[:, par, c, 2 + k:2 + k + TT],
                            start=(k == 0), stop=(k == 30)))(),
                            reads=["diag%d" % (di // 34), "zb%d_%d" % (par, c), "zbh%d" % par], writes=[bk(b)],
                            signal=(k == 30))
                    S.op("act", (lambda c=c, b=b, hf=hf: lambda e: e.activation(
                        out=zc[:, c, :], in_=hb(b, hf), func=AF.Identity, bias=sm[:, SM_CBB + c:SM_CBB + c + 1]))(),
                        reads=["small"], writes=["zc%d" % c, bk(b)])
                    S.op("act", (lambda c=c, b=b, hf=hf: lambda e: e.activation(
                        out=zq[:, c, :], in_=hb(b, hf), func=AF.Square, bias=sm[:, SM_CBB + c:SM_CBB + c + 1]))(),
                        reads=["small"], writes=["zq%d" % c, bk(b)])

            def B2_stats():
                S.cur = "B2"
                for c in range(4):
                    S.op("pe", (lambda c=c: lambda e: e.matmul(hb(6, 0), lhsT=ones[:], rhs=zc[:, c, :],
                                                              start=(c == 0), stop=(c == 3)))(),
                         reads=["ones", "zc%d" % c], writes=[bk(6)], signal=(c == 3))
                for c in range(4):
                    S.op("pe", (lambda c=c: lambda e: e.matmul(hb(6, 1), lhsT=ones[:], rhs=zq[:, c, :],
                                                              start=(c == 0), stop=(c == 3)))(),
                         reads=["ones", "zq%d" % c], writes=[bk(6)], signal=(c == 3))
                S.op("dve", lambda e: e.tensor_scalar(out=mu[:], in0=hb(6, 0), scalar1=1.0 / 512, scalar2=None,
                                                      op0=ALU.mult),
                     writes=["mu", bk(6)])
                S.op("dve", lambda e: e.tensor_tensor(out=v1[:], in0=mu[:], in1=mu[:], op=ALU.mult),
                     reads=["mu"], writes=["v1"])
                S.op("dve", lambda e: e.scalar_tensor_tensor(out=v1[:], in0=hb(6, 1), scalar=1.0 / 512, in1=v1[:],
                                                             op0=ALU.mult, op1=ALU.subtract),
                     reads=["v1"], writes=["v1", bk(6)])
                S.op("act", lambda e: e.activation(out=v1[:], in_=v1[:], func=AF.Ln, bias=epsc[:, 1:2]),
                     reads=["v1", "epsc"], writes=["v1"])
                S.op("act", lambda e: e.activation(out=rstdB[:], in_=v1[:], func=AF.Exp, scale=-0.5),
                     reads=["v1"], writes=["rstdB"])
                S.op("dve", lambda e: e.scalar_tensor_tensor(out=nmr[:], in0=mu[:], scalar=-1.0, in1=rstdB[:],
                                                             op0=ALU.mult, op1=ALU.mult),
                     reads=["mu", "rstdB"], writes=["nmr"])

            def B2_chunk(c):
                S.cur = "B2"
                S.op("dve", (lambda c=c: lambda e: e.tensor_tensor(out=t1[:, c % 2, :], in0=zc[:, c, :], in1=rstdB[:],
                                                                   op=ALU.mult))(),
                     reads=["zc%d" % c, "rstdB"], writes=["t1%d" % (c % 2)])
                S.op("dve", (lambda c=c: lambda e: e.tensor_tensor(out=t2[:, c % 2, :], in0=t1[:, c % 2, :],
                                                                   in1=nmr[:], op=ALU.add))(),
                     reads=["t1%d" % (c % 2), "nmr"], writes=["t2%d" % (c % 2)])
                S.op("act", (lambda c=c: lambda e: e.activation(
                    out=ycat_b[:, c, :], in_=t2[:, c % 2, :], func=AF.Silu,
                    scale=sm[:, SM_LNG + c:SM_LNG + c + 1], bias=sm[:, SM_LNB + c:SM_LNB + c + 1]))(),
                    reads=["t2%d" % (c % 2), "small"], writes=["yb%d" % c])

            def O(t, par):
                yfn = lambda kc: ycat_a[:, par, kc, :] if kc < 4 else ycat_b[:, kc - 4, :]
                yk = ["ya%d_%d" % (par, c) for c in range(4)] + ["yb%d" % c for c in range(4)]
                outproj(w_out, "w_out", yfn, yk, first_bank=(rr[0] + 1) % 4, korder=(4, 5, 6, 7, 0, 1, 2, 3))
                PN(t, 1)

            hx = lambda c: xh_sb[:, c, :]
            PA(hx, ["xh"] * 8, HALO, sq8_1)
            PB(HALO, sq8_1)
            PA(xfn(0), xkeys(0), TT, sq8_1)
            PC(hx, ["xh"] * 8, HALO, 0, halo=True)
            PB(TT, sq8_1)
            PC(xfn(0), xkeys(0), TT, 0)
            F(HALO, 0, 0, True, parts=("cv",))
            F(TT, 0, HALO, False, parts=("cv",))
            F(HALO, 0, 0, True, parts=("glu",))
            F(TT, 0, HALO, False, parts=("glu",))
            F(TT, 0, HALO, False, parts=("a",))
            S.alias(["ysq0", "ysq1"], ["hh_%d" % c for c in range(8)])
            for t in range(NT):
                par = t % 2
                nxt = t + 1 < NT
                if nxt:
                    HC(par)
                    PA(xfn(t + 1), xkeys(t + 1), TT, sq8_1)
                elif NPHASES >= 2:
                    PA(xfn(0), xkeys(0), TT, sq8_1)
                B1(par, [0, 1] if t == 0 else [1, 2])
                if nxt:
                    PB(TT, sq8_1)
                    PC(xfn(t + 1), xkeys(t + 1), TT, 0)
                elif NPHASES >= 2:
                    PB(TT, sq8_1)
                    PC(xfn(0), xkeys(0), TT, 2)
                    PA(lambda c: memT[:, c, :], ["memT"] * 8, NMEM, sq8_1)
                B1(par, [2, 3] if t == 0 else [3])
                if not nxt and NPHASES >= 2:
                    PB(NMEM, sq8_1)
                    S.alias(["wq%d" % k for k in range(8)] + ["wo%d" % k for k in range(8)],
                            ["diag%d" % p for p in range(4)])
                    load_w(wq, wq_d, "wq")
                    load_w(wo, wo_d, "wo")
                B2_stats()
                if nxt:
                    F(TT, 1 - par, HALO, False, cs=(0,))
                    B2_chunk(0)
                    F(TT, 1 - par, HALO, False, cs=(1,))
                    B2_chunk(1)
                    F(TT, 1 - par, HALO, False, cs=(2,))
                    B2_chunk(2)
                    B2_chunk(3)
                    F(TT, 1 - par, HALO, False, cs=(3,))
                    B1(1 - par, [0])
                    if t + 2 == NT and NPHASES >= 2:
                        S.alias(["wk%d" % k for k in range(8)] + ["wv%d" % k for k in range(8)], w_in_keys)
                        load_w(wk, wk_d, "wk")
                        load_w(wv, wv_d, "wv")
                        S.alias(["memT"], w_in_keys)
                        S.dma("sp", "memT", lambda e: e.dma_start(out=memT[:], in_=memT_d), writes=["memT"])
                else:
                    for c in range(4):
                        B2_chunk(c)
                    if NPHASES >= 2:
                        norm_mem()
                O(t, par)

        def run_sp2():
            S.alias(sp2_keys + ["sq8_%d" % c for c in range(8)], sp1_keys)
            rr = [0]

            def sbank():
                rr[0] = (rr[0] + 1) % 3
                return 4 + rr[0]

            hfn = lambda kc: hh[0][:, kc, :]

            def Q(e2s=(0, 1, 2, 3)):
                S.cur = "Q"
                for e2 in e2s:
                    b = sbank()
                    for hf in range(2):
                        ec = e2 * 2 + hf
                        proj(b, hf, TT, wq, "wq", ec * 128, hfn, h0k)
                    S.op("act", (lambda e2=e2, b=b: lambda e: e.activation(
                        out=qT[:, 2 * e2:2 * e2 + 2, :], in_=ps[b][:, :].rearrange("p (a n) -> p a n", a=2),
                        func=AF.Copy))(),
                        reads=[], writes=["qT%d" % (2 * e2), "qT%d" % (2 * e2 + 1), bk(b)])

            Q()
            if NT > 1:
                PA(xfn(1), xkeys(1), TT, sq8_2)
            S.cur = "KV"
            for e2 in range(4):
                b = sbank()
                for hf in range(2):
                    ec = e2 * 2 + hf
                    proj(b, hf, NMEM, wk, "wk", ec * 128, lambda kc: hm[:, kc, :], ["hm"] * 8)
                S.op("act", (lambda e2=e2, b=b: lambda e: e.activation(
                    out=KT[:, 2 * e2:2 * e2 + 2, :], in_=ps[b][:, :].rearrange("p (a n) -> p a n", a=2), func=AF.Copy))(),
                    reads=[], writes=["KT", bk(b)])
            if NT > 1:
                PB(TT, sq8_2)
                PC(xfn(1), xkeys(1), TT, 2)
                S.cur = "KV"
            for mc in range(2):
                for eh in range(2):
                    b = sbank()
                    for kc in range(8):
                        S.op("pe", (lambda kc=kc, mc=mc, eh=eh, b=b: lambda e: e.matmul(
                            ps[b][:, :], lhsT=hm[:, kc, mc * 128:(mc + 1) * 128], rhs=wv[:, kc, eh * 512:(eh + 1) * 512],
                            start=(kc == 0), stop=(kc == 7)))(),
                            reads=["hm", "wv%d" % kc], writes=[bk(b)], signal=(kc == 7))
                    S.op("act", (lambda mc=mc, eh=eh, b=b: lambda e: e.activation(
                        out=Vt[:, mc, eh * 512:(eh + 1) * 512], in_=ps[b][:, :], func=AF.Copy))(),
                        reads=[], writes=["V", bk(b)])
            S.alias(["qT%d" % c for c in range(8)] + ["p%d" % i for i in range(4)] + ["rs%d" % i for i in range(4)],
                    ["memT", "hm"])
            S.alias(["wg%d" % k for k in range(8)],
                    ["wk%d" % k for k in range(8)] + ["wv%d" % k for k in range(8)] +
                    w_in_keys + ["w_out%d" % k for k in range(8)])
            load_w(wg, wg_d, "wg")
            S.alias(["wu%d" % j for j in range(0, 7)], ["w_out%d" % k for k in range(8)] + ["diag%d" % p for p in range(4)])
            load_wu(0, 7, "wu_a")

            def SC():
                S.cur = "SC"
                for hd in range(4):
                    b = sbank()
                    for mc in range(2):
                        for j in range(2):
                            ec = 2 * hd + j
                            S.op("pe", (lambda mc=mc, j=j, ec=ec, b=b: lambda e: e.matmul(
                                hb(b, mc), lhsT=KT[:, ec, mc * 128:(mc + 1) * 128], rhs=qT[:, ec, :],
                                start=(j == 0), stop=(j == 1)))(),
                                reads=["KT", "qT%d" % ec], writes=[bk(b)], signal=(j == 1))
                    S.op("act", (lambda hd=hd, b=b: lambda e: e.activation(
                        out=pbuf[:, hd, :, :], in_=ps[b][:, :].rearrange("p (a n) -> p a n", a=2), func=AF.Exp,
                        scale=1.0 / 16.0))(),
                        reads=[], writes=["p%d" % hd, bk(b)])

            def SM_PV(sb0=4):
                S.cur = "SM_PV"
                for hd in range(4):
                    for mc in range(2):
                        S.op("pe", (lambda mc=mc, hd=hd: lambda e: e.matmul(
                            hb(sb0 + hd // 2, hd % 2), lhsT=ones[:], rhs=pbuf[:, hd, mc, :], start=(mc == 0), stop=(mc == 1)))(),
                            reads=["ones", "p%d" % hd], writes=[bk(sb0 + hd // 2)], signal=(mc == 1))
                for hd in range(4):
                    for dc in range(2):
                        ec = 2 * hd + dc
                        for mc in range(2):
                            S.op("pe", (lambda mc=mc, ec=ec, dc=dc, hd=hd: lambda e: e.matmul(
                                hb(hd, dc), lhsT=Vt[:, mc, ec * 128:(ec + 1) * 128], rhs=pbuf[:, hd, mc, :],
                                start=(mc == 0), stop=(mc == 1)))(),
                                reads=["V", "p%d" % hd], writes=[bk(hd)], signal=(mc == 1))
                for h2 in range(2):
                    S.op("act", (lambda h2=h2: lambda e: e.activation(
                        out=rs[:, 2 * h2:2 * h2 + 2, :], in_=ps[sb0 + h2][:, :].rearrange("p (a n) -> p a n", a=2),
                        func=AF.Ln))(),
                        reads=[], writes=["rs%d" % (2 * h2), "rs%d" % (2 * h2 + 1), bk(sb0 + h2)])
                    S.op("act", (lambda h2=h2: lambda e: e.activation(
                        out=rs[:, 2 * h2:2 * h2 + 2, :], in_=rs[:, 2 * h2:2 * h2 + 2, :], func=AF.Exp, scale=-1.0))(),
                        reads=["rs%d" % (2 * h2), "rs%d" % (2 * h2 + 1)], writes=["rs%d" % (2 * h2), "rs%d" % (2 * h2 + 1)])
                for hd in range(4):
                    for dc in range(2):
                        ec = 2 * hd + dc
                        S.op("dve", (lambda ec=ec, dc=dc, hd=hd: lambda e: e.tensor_tensor(
                            out=oT[:, ec, :], in0=hb(hd, dc), in1=rs[:, hd, :], op=ALU.mult))(),
                            reads=["rs%d" % hd], writes=["oT%d" % ec, bk(hd)])

            def O(t):
                outproj(wo, "wo", lambda kc: oT[:, kc, :], ["oT%d" % c for c in range(8)])
                PN(t, 3)

            def PRE(t):
                PA(xfn(t), xkeys(t), TT, sq8_2)
                PB(TT, sq8_2)
                PC(xfn(t), xkeys(t), TT, 2)

            SC()
            for t in range(NT):
                if t + 2 < NT:
                    PA(xfn(t + 2), xkeys(t + 2), TT, sq8_2)
                elif t + 2 == NT and NPHASES >= 3:
                    PA(xfn(0), xkeys(0), TT, sq8_2)
                    S.alias(["wdj%d" % i for i in range(4)], ["memT"] + sp1_keys)
                    for g in range(4):
                        load_wd(g)
                if t + 1 < NT:
                    Q()
                    if t + 2 == NT and NPHASES >= 3:
                        S.alias(["wu%d" % j for j in range(7, 15)], ["wq%d" % k for k in range(8)])
                        load_wu(7, 15, "wu_b")
                if t + 1 == NT and NPHASES >= 3:
                    gu_pe(0, 0)
                    gu_pe(0, 1)
                    SM_PV(sb0=6)
                    S.alias(["sg0", "sg1", "aj0", "aj1"], ["qT%d" % c for c in range(8)] + sp1_keys)
                    gu_ew(0, 0)
                    gu_ew(0, 1)
                else:
                    SM_PV()
                if t + 1 < NT:
                    SC()
                if t + 2 < NT:
                    PB(TT, sq8_2)
                    PC(xfn(t + 2), xkeys(t + 2), TT, 2)
                elif t + 2 == NT and NPHASES >= 3:
                    PB(TT, sq8_2)
                    PC(xfn(0), xkeys(0), TT, 4, hp=0)
                O(t)

        def run_sp3():
            sp3_keys = ["sg0", "sg1"] + ["aj%d" % i for i in range(4)] + ["wdj%d" % i for i in range(NS)] + \
                       ["h1_%d" % c for c in range(8)] + ["sq8_%d" % c for c in range(8)] + \
                       ["ysq%d" % c for c in range(8)] + ["tmp%d" % c for c in range(4)]
            S.alias(sp3_keys, sp2_keys + sp1_keys)
            NG = NT * NJ

            for g in range(4, NS):
                load_wd(g)
            S.alias(["wu%d" % j for j in range(15, NJ)], ["wo%d" % k for k in range(8)])
            load_wu(15, NJ, "wu_c")

            def down(t, j):
                S.cur = "down"
                g = t * NJ + j
                slot = g % NS
                for oc in range(8):
                    b, hf = ybank(oc)
                    S.op("pe", (lambda oc=oc, b=b, hf=hf, slot=slot, j=j: lambda e: e.matmul(
                        hb(b, hf), lhsT=wd_slot_cols(slot, oc * 128, (oc + 1) * 128), rhs=actj[:, j % 4, :],
                        start=(j == 0 and oc < 4), stop=(j == NJ - 1), skip_group_check=True))(),
                        reads=["wdj%d" % slot, "aj%d" % (j % 4)], writes=[bk(b)], signal=(oc == 7))
                if g + NS < NG:
                    load_wd(g + NS)

            def PRE_A(t):
                PA(xfn(t), xkeys(t), TT, sq8_3)

            def PRE_B(t):
                PB(TT, sq8_3)

            def PRE_C(t):
                PC(xfn(t), xkeys(t), TT, 4, hp=t % 2)

            out_toks = []
            LAG = 3
            if NT > 1:
                PRE_A(1)

            def finish_tile(t):
                S.cur = "post_stats"
                for i, c in enumerate(reversed(range(8))):
                    S.op("pe", (lambda c=c, i=i: lambda e: e.matmul(hb(7, 1), lhsT=ones[:], rhs=ysq8[:, c, :],
                                                                   start=(i == 0), stop=(i == 7)))(),
                         reads=["ones", "ysq%d" % c], writes=[bk(7)], signal=(i == 7))
                ps_finish()
                last = (t + 1 == NT)
                PN(t, 5, tbuf=tmp4, nt=4, last=last)
                if not last:
                    out_toks.append(S.dma("sp", "out",
                                          (lambda t=t: lambda e: e.dma_start(out=outT_d[:, :, t * TT:(t + 1) * TT],
                                                                             in_=x_sb[:, :, t * TT:(t + 1) * TT]))(),
                                          reads=["x%d_%d" % (c, t) for c in range(8)]))
                else:
                    for c2 in range(4):
                        out_toks.append(S.dma("sp", "out",
                                              (lambda t=t, c2=c2: lambda e: e.dma_start(
                                                  out=outT_d[:, 2 * c2:2 * c2 + 2, t * TT:(t + 1) * TT],
                                                  in_=x_sb[:, 2 * c2:2 * c2 + 2, t * TT:(t + 1) * TT]))(),
                                              reads=["x%d_%d" % (c, t) for c in (2 * c2, 2 * c2 + 1)]))

            for t in range(NT):
                hp = t % 2
                hk = ["h%d_%d" % (hp, c) for c in range(8)]
                hfn = lambda kc, hp=hp: hh[hp][:, kc, :]
                for j in range(NJ):
                    if j == 1 and t >= 1:
                        finish_tile(t - 1)
                    if not (t == 0 and j < 2):
                        gu_pe(t, j)
                        gu_ew(t, j)
                    if j == LAG and t + 1 < NT:
                        PRE_B(t + 1)
                        S.cur = "down"
                    if j >= LAG:
                        down(t, j - LAG)
                    if j == 10 and t + 2 < NT:
                        PRE_A(t + 2)
                    elif j == 12 and t + 1 < NT:
                        PRE_C(t + 1)
                for jj in range(NJ - LAG, NJ):
                    down(t, jj)
                S.cur = "post_stats"
                if t + 1 == NT:
                    S.op("act", lambda e: e.activation(out=ms_post[:, 0:1], in_=epsc[:, 0:1], func=AF.Ln),
                         reads=["epsc"], writes=["ms_post"])
                for c in range(8):
                    ps_square(c, ysq8, 8)
                if t + 1 == NT:
                    finish_tile(t)
            return out_toks

        out_toks = []
        if NPHASES >= 1:
            run_sp1()
        if NPHASES >= 2:
            run_sp2()
        if NPHASES >= 3:
            out_toks = run_sp3()
        else:
            for t in range(NT):
                out_toks.append(S.dma("sp", "out",
                                      (lambda t=t: lambda e: e.dma_start(out=outT_d[:, :, t * TT:(t + 1) * TT],
                                                                         in_=x_sb[:, :, t * TT:(t + 1) * TT]))(),
                                      reads=["x%d_%d" % (c, t) for c in range(8)]))
        S.wait_all("sp", out_toks)

        with nc.Block() as block:
            S.emit(block)
    return nc


_PROGRAM = None
_LAST_SCHED = None


def kernel(x, mem, mix_pre_g, w_mix_in, conv_a_w, conv_b_w, conv_b_b, ln_b_g, ln_b_b,
           w_mix_out, mix_post_g, xa_pre_g, mem_norm_g, w_q, w_k, w_v, w_o, xa_post_g,
           ffn_pre_g, w_gate, w_up, w_down, ffn_post_g):
    global _PROGRAM
    f = lambda a: np.ascontiguousarray(np.asarray(a, dtype=np.float32))
    x = f(x).reshape(SEQ, D)
    xp = np.concatenate([np.zeros((HALO, D), np.float32), x], axis=0)
    in_maps = []
    vec8 = lambda v: f(v).reshape(8, 128).T
    vec4 = lambda v: f(v).reshape(4, 128).T
    small = np.zeros((128, SM_N), np.float32)
    for i, g in enumerate([mix_pre_g, mix_post_g, xa_pre_g, xa_post_g, ffn_pre_g, ffn_post_g]):
        small[:, SM_G + i * 8:SM_G + (i + 1) * 8] = vec8(g)
    small[:, SM_MEMG:SM_MEMG + 8] = vec8(mem_norm_g)
    small[:, SM_CBB:SM_CBB + 4] = vec4(conv_b_b)
    small[:, SM_LNG:SM_LNG + 4] = vec4(ln_b_g)
    small[:, SM_LNB:SM_LNB + 4] = vec4(ln_b_b)
    caw = f(conv_a_w)
    cbw = f(conv_b_w)
    for k in range(3):
        small[:, SM_CAW + k * 4:SM_CAW + (k + 1) * 4] = vec4(caw[k])
    for k in range(31):
        small[:, SM_CBW + k * 4:SM_CBW + (k + 1) * 4] = vec4(cbw[k])
    ident = np.eye(128, dtype=np.float32)
    memT = np.ascontiguousarray(f(mem).reshape(NMEM, 8, 128).transpose(2, 1, 0))
    shared = {
        "small": small, "ident": ident, "memT": memT,
        "w_mix_in": f(w_mix_in), "w_mix_out": f(w_mix_out), "w_q": f(w_q), "w_k": f(w_k), "w_v": f(w_v),
        "w_o": f(w_o), "w_gate": f(w_gate), "w_up": f(w_up), "w_down": f(w_down),
    }
    for c in range(NCORES):
        seg = xp[c * TPC:c * TPC + HALO + TPC]
        xT = np.ascontiguousarray(seg.reshape(HALO + TPC, 8, 128).transpose(2, 1, 0))
        m = dict(shared)
        m["xT"] = xT
        in_maps.append(m)
    if _PROGRAM is None:
        _PROGRAM = build_program()
    res = run_bass_kernel_spmd(_PROGRAM, in_maps, core_ids=list(range(NCORES)))
    out = np.empty((SEQ, D), np.float32)
    for c in range(NCORES):
        oT = np.asarray(res.results[c]["outT"]).reshape(128, 8, TPC)
        out[c * TPC:(c + 1) * TPC] = oT.transpose(2, 1, 0).reshape(TPC, D)
    return out.reshape(1, SEQ, D)
```
